# Optimizing a Trainium2 kernel written in Bass

```python
import math
import jax, jax.numpy as jnp
from jax import lax
import numpy as np

D_MODEL = 2048
BATCH = 1
SEQ = 16384
DEPTH = 1
DEC_BATCH = 1
DEC_SEQ = 8192
PAST_LEN = 128

GRID_W = 64
PLE_DIM = 256
F_WIDTH = 1024
F_GROUPS = 8
F_GROUP_CH = F_WIDTH // F_GROUPS
NA_HEADS = 8
NA_HEAD_DIM = 128
NA_WIDTH = NA_HEADS * NA_HEAD_DIM
WIN_H = 8
WIN_W = 16
D_FF = 5632
CONV_W = 3
IN_COLS = F_WIDTH + 3 * NA_WIDTH + 2 * D_MODEL
LN_EPS = 1e-5
DN_ALPHA = (2.0 * DEPTH) ** 0.25
DN_BETA = (8.0 * DEPTH) ** -0.25

kernel_name = "hybrid_fnet_natten_convffn_encoder"


def layer_norm(x, g, b):
    xf = x.astype(jnp.float32)
    mu = jnp.mean(xf, axis=-1, keepdims=True)
    var = jnp.mean(jnp.square(xf - mu), axis=-1, keepdims=True)
    y = (xf - mu) * lax.rsqrt(var + LN_EPS) * g.astype(jnp.float32) + b.astype(jnp.float32)
    return y.astype(x.dtype)


def fourier_mix(f_in):
    bsz, s, _ = f_in.shape
    f = f_in.reshape(bsz, s, F_GROUPS, F_GROUP_CH).astype(jnp.float32)
    fr = jnp.fft.fft2(f, axes=(1, 3), norm="ortho").real
    return fr.reshape(bsz, s, F_WIDTH).astype(f_in.dtype)


def neighbourhood_attention(q, k, v, rpb):
    bsz, s, _ = q.shape
    rows = s // GRID_W
    kh = min(WIN_H, rows)
    shp = (bsz, rows, GRID_W, NA_HEADS, NA_HEAD_DIM)
    qg, kg, vg = q.reshape(shp), k.reshape(shp), v.reshape(shp)
    scale = NA_HEAD_DIM ** -0.5
    col_start = np.clip(np.arange(GRID_W) - WIN_W // 2, 0, GRID_W - WIN_W)
    col_idx = col_start[:, None] + np.arange(WIN_W)[None, :]
    dc = col_idx - np.arange(GRID_W)[:, None]
    col_idx = jnp.asarray(col_idx, dtype=jnp.int32)
    dc_idx = jnp.asarray(dc + WIN_W - 1, dtype=jnp.int32)
    rpb_f = rpb.astype(jnp.float32)

    def row_fn(r):
        rs = jnp.clip(r - kh // 2, 0, rows - kh)
        q_r = lax.dynamic_index_in_dim(qg, r, axis=1, keepdims=False)
        k_rows = lax.dynamic_slice_in_dim(kg, rs, kh, axis=1)
        v_rows = lax.dynamic_slice_in_dim(vg, rs, kh, axis=1)
        k_win = k_rows[:, :, col_idx]
        v_win = v_rows[:, :, col_idx]
        sc = jnp.einsum('bqhd,bkqwhd->bhqkw', q_r, k_win).astype(jnp.float32) * scale
        dr = rs + jnp.arange(kh, dtype=jnp.int32) - r
        bias = rpb_f[:, dr[:, None, None] + WIN_H - 1, dc_idx[None, :, :]]
        sc = sc + jnp.transpose(bias, (0, 2, 1, 3))[None]
        p = jax.nn.softmax(sc.reshape(bsz, NA_HEADS, GRID_W, kh * WIN_W), axis=-1)
        p = p.reshape(bsz, NA_HEADS, GRID_W, kh, WIN_W).astype(v.dtype)
        return jnp.einsum('bhqkw,bkqwhd->bqhd', p, v_win)

    out = lax.map(row_fn, jnp.arange(rows, dtype=jnp.int32))
    return jnp.transpose(out, (1, 0, 2, 3, 4)).reshape(bsz, s, NA_WIDTH)


def conv3_centred(u, w, b):
    up = jnp.pad(u, ((0, 0), (1, 1), (0, 0)))
    return up[:, :-2] * w[0] + up[:, 1:-1] * w[1] + up[:, 2:] * w[2] + b


def encoder_layer(x, p, w_in, gate_b, fourier_w, natten_rpb, natten_w, w_out, ln1_g, ln1_b,
                  ffn_up, ffn_conv, ffn_conv_b, ffn_down, ple_proj, ple_gate, ln2_g, ln2_b):
    z = x @ w_in
    f_in, q, k, v, g_a, g_b = jnp.split(
        z, [F_WIDTH, F_WIDTH + NA_WIDTH, F_WIDTH + 2 * NA_WIDTH, F_WIDTH + 3 * NA_WIDTH,
            F_WIDTH + 3 * NA_WIDTH + D_MODEL], axis=-1)
    gates = jax.nn.sigmoid(jnp.concatenate([g_a, g_b], axis=-1) + gate_b)
    branch_a = fourier_mix(f_in) @ fourier_w
    branch_b = neighbourhood_attention(q, k, v, natten_rpb) @ natten_w
    merged = gates[..., :D_MODEL] * branch_a + gates[..., D_MODEL:] * branch_b
    x = layer_norm(DN_ALPHA * x + merged @ w_out, ln1_g, ln1_b)
    u = conv3_centred(x @ ffn_up, ffn_conv, ffn_conv_b)
    h = jax.nn.gelu(u[..., :D_FF]) * u[..., D_FF:]
    ffn_out = h @ ffn_down
    ple = jax.nn.sigmoid(x @ ple_gate) * (p @ ple_proj)
    return layer_norm(DN_ALPHA * x + ffn_out + ple, ln2_g, ln2_b)


def setup_inputs(seed: int = 0) -> dict:
    key = jax.random.key(seed)
    ks = jax.random.split(key, 20)
    nrm = lambda k, shp, s: jax.random.normal(k, shp, jnp.float32) * s
    return {
        "x_prompt": nrm(ks[0], (BATCH, SEQ, D_MODEL), 1.0),
        "x_sample": nrm(ks[1], (DEC_BATCH, DEC_SEQ, D_MODEL), 1.0),
        "p_prompt": nrm(ks[2], (DEPTH, BATCH, SEQ, PLE_DIM), 1.0),
        "p_sample": nrm(ks[3], (DEPTH, DEC_BATCH, DEC_SEQ, PLE_DIM), 1.0),
        "w_in": nrm(ks[4], (DEPTH, D_MODEL, IN_COLS), D_MODEL ** -0.5),
        "gate_b": nrm(ks[5], (DEPTH, 2 * D_MODEL), 0.01),
        "fourier_w": nrm(ks[6], (DEPTH, F_WIDTH, D_MODEL), F_WIDTH ** -0.5),
        "natten_rpb": nrm(ks[7], (DEPTH, NA_HEADS, 2 * WIN_H - 1, 2 * WIN_W - 1), 0.1),
        "natten_w": nrm(ks[8], (DEPTH, NA_WIDTH, D_MODEL), NA_WIDTH ** -0.5),
        "w_out": nrm(ks[9], (DEPTH, D_MODEL, D_MODEL), DN_BETA * D_MODEL ** -0.5),
        "ln1_g": 1.0 + nrm(ks[10], (DEPTH, D_MODEL), 0.02),
        "ln1_b": nrm(ks[11], (DEPTH, D_MODEL), 0.02),
        "ffn_up": nrm(ks[12], (DEPTH, D_MODEL, 2 * D_FF), D_MODEL ** -0.5),
        "ffn_conv": nrm(ks[13], (DEPTH, CONV_W, 2 * D_FF), CONV_W ** -0.5),
        "ffn_conv_b": nrm(ks[14], (DEPTH, 2 * D_FF), 0.02),
        "ffn_down": nrm(ks[15], (DEPTH, D_FF, D_MODEL), DN_BETA * D_FF ** -0.5),
        "ple_proj": nrm(ks[16], (DEPTH, PLE_DIM, D_MODEL), PLE_DIM ** -0.5),
        "ple_gate": nrm(ks[17], (DEPTH, D_MODEL, D_MODEL), D_MODEL ** -0.5),
        "ln2_g": 1.0 + nrm(ks[18], (DEPTH, D_MODEL), 0.02),
        "ln2_b": nrm(ks[19], (DEPTH, D_MODEL), 0.02),
    }


def reference(x_prompt, x_sample, p_prompt, p_sample, w_in, gate_b, fourier_w, natten_rpb, natten_w,
              w_out, ln1_g, ln1_b, ffn_up, ffn_conv, ffn_conv_b, ffn_down, ple_proj, ple_gate,
              ln2_g, ln2_b):
    xp, xs = x_prompt, x_sample
    for i in range(DEPTH):
        xp = encoder_layer(xp, p_prompt[i], w_in[i], gate_b[i], fourier_w[i], natten_rpb[i], natten_w[i],
                           w_out[i], ln1_g[i], ln1_b[i], ffn_up[i], ffn_conv[i], ffn_conv_b[i], ffn_down[i],
                           ple_proj[i], ple_gate[i], ln2_g[i], ln2_b[i])
        xs = encoder_layer(xs, p_sample[i], w_in[i], gate_b[i], fourier_w[i], natten_rpb[i], natten_w[i],
                           w_out[i], ln1_g[i], ln1_b[i], ffn_up[i], ffn_conv[i], ffn_conv_b[i], ffn_down[i],
                           ple_proj[i], ple_gate[i], ln2_g[i], ln2_b[i])
    return (xp, xs)
```

```python
import numpy as np
import ml_dtypes
import concourse.bass as bass
import concourse.mybir as mybir
from concourse.bass_utils import run_bass_kernel_spmd

F32 = mybir.dt.float32
BF16 = mybir.dt.bfloat16
AF = mybir.ActivationFunctionType
ALU = mybir.AluOpType
NPBF = ml_dtypes.bfloat16

NCORES = 8
D = 2048
KC = 16
NEG = -30000.0
ALPHA = 2.0 ** 0.25
SEQ_S = [16384, 8192]
SEQ_OWN = [2048, 1024]
SEQ_ROWS = [256, 128]
SEQ_P = [128, 64]
SEQ_NCC = [18, 10]
E64 = [o + 128 for o in SEQ_OWN]
KVT = [o + 640 for o in SEQ_OWN]
KV_E0 = 320
NQT = [e // 128 for e in E64]
OWN_OFF = [0, 2048]
SB_BASE = 16640
SB_LIMIT = 229000


class Op:
    __slots__ = ("eng", "fn", "deps", "dma", "sig", "val", "sem", "idx")


class Prog:
    ENGS = ("pe", "act", "dve", "pool", "sp")
    KPOOL = 8

    def __init__(self):
        self.ops = []
        self.lastw = {}
        self.readers = {}
        self.eng_last = {}
        self.pending = {}
        self.barrier_set = set()
        self.dma_cnt = {e: 0 for e in self.ENGS}
        self.pool_last = {}

    def add(self, eng, fn, reads=(), writes=(), dma=False):
        idx = len(self.ops)
        deps = set()
        for k in reads:
            w = self.lastw.get(k)
            if w is not None:
                deps.add(w)
        for k in writes:
            w = self.lastw.get(k)
            if w is not None:
                deps.add(w)
            for r in self.readers.get(k, ()):
                deps.add(r)
        if self.pending.get(eng):
            deps |= self.barrier_set
            self.pending[eng] = False
        op = Op()
        op.eng, op.fn, op.dma, op.sig, op.val, op.sem, op.idx = eng, fn, dma, False, 0, None, idx
        if dma:
            n = self.dma_cnt[eng]
            self.dma_cnt[eng] = n + 1
            slot = (eng, n % self.KPOOL)
            prev = self.pool_last.get(slot)
            if prev is not None:
                deps.add(prev)
            self.pool_last[slot] = idx
            op.sem = slot
            op.val = 16 * (n // self.KPOOL + 1)
            op.sig = True
        op.deps = deps
        self.ops.append(op)
        for k in reads:
            self.readers.setdefault(k, []).append(idx)
        for k in writes:
            self.lastw[k] = idx
            self.readers[k] = []
        if not dma:
            self.eng_last[eng] = idx
        return idx

    def barrier(self):
        s = set(self.eng_last.values()) | set(self.pool_last.values())
        self.barrier_set = s
        for e in self.ENGS:
            self.pending[e] = True

    def finalize(self, nc, sems):
        self.barrier()
        self.add("sp", None)
        for op in self.ops:
            for d in op.deps:
                dop = self.ops[d]
                if dop.dma:
                    continue
                if dop.eng == "pe" and op.eng == "pe" and not op.dma:
                    continue
                dop.sig = True
        cnt = {e: 0 for e in self.ENGS}
        for op in self.ops:
            if not op.dma and op.sig:
                cnt[op.eng] += 1
                op.val = cnt[op.eng]
                op.sem = ("c", op.eng)
        self.by_eng = {e: [op for op in self.ops if op.eng == e] for e in self.ENGS}
        self.sems = sems

    def emit(self, eng, e):
        seen = {}
        for op in self.by_eng[eng]:
            waits = {}
            for d in op.deps:
                dop = self.ops[d]
                if (not dop.dma) and dop.eng == "pe" and eng == "pe" and not op.dma:
                    continue
                if dop.val > waits.get(dop.sem, 0):
                    waits[dop.sem] = dop.val
            for s, v in waits.items():
                if seen.get(s, 0) >= v:
                    continue
                seen[s] = v
                e.wait_ge(self.sems[s], v)
            if op.fn is None:
                continue
            ins = op.fn(e)
            if op.sig:
                ins.then_inc(self.sems[op.sem], 16 if op.dma else 1)


class Arena:
    def __init__(self, nc):
        self.nc = nc
        self.off = SB_BASE
        self.n = 0
        self.memo = None
        self.replay = None

    def start_group(self):
        self.memo = {}
        self.replay = None

    def next_member(self):
        self.replay = {k: list(v) for k, v in self.memo.items()}

    def end_group(self):
        self.memo = None
        self.replay = None

    def mark(self):
        return self.off

    def reset(self, m):
        self.off = m

    def alloc(self, shape, dtype):
        key = (tuple(shape), str(dtype))
        if self.replay is not None and self.replay.get(key):
            return self.replay[key].pop(0)
        t = self._alloc(shape, dtype)
        if self.memo is not None:
            self.memo.setdefault(key, []).append(t)
        return t

    def _alloc(self, shape, dtype):
        sz = 4 if dtype == F32 else 2
        nb = int(np.prod(shape[1:])) * sz
        nb = (nb + 31) // 32 * 32
        assert self.off + nb <= SB_LIMIT, ("SBUF overflow", self.off, nb)
        self.n += 1
        t = self.nc.alloc_sbuf_tensor_at("sb%d" % self.n, list(shape), dtype, offset=self.off)
        self.off += nb
        return t


def build_program():
    nc = bass.Bass("TRN2", target_bir_lowering=False)
    P = Prog()
    A = Arena(nc)

    def din(name, shape, dt=F32):
        return nc.dram_tensor(name, list(shape), dt, kind="ExternalInput")

    def dscr(name, shape, dt=BF16):
        return nc.dram_tensor(name, list(shape), dt, kind="Internal")

    xfa = din("xfa", [192, 128, KC, 128])
    tfa = din("tfa", [192, 128, 256], BF16)
    win_f = din("win_f", [128, KC, 1024])
    win_c = din("win_c", [64, 128, KC, 128])
    win_v = din("win_v", [2, 128, KC, 512])
    fw = din("fw", [128, 8, 2048])
    nw = din("nw", [16, 128, 8, 128])
    wout = din("wout", [16, 128, KC, 128])
    fup = din("fup", [88, 128, KC, 128])
    fdn = din("fdn", [16, 128, 44, 128])
    pg = din("pg", [16, 128, KC, 128])
    pp = din("pp", [16, 128, 2, 128])
    vecs = din("vecs", [128, 32 + 64 + 88 * 4])
    ccs = din("ccs", [128, 256])
    bias_t = din("bias_t", [128, 8, 1024], BF16)
    ident_f_d = din("ident_f", [128, 128])
    cbf_d = din("cbf", [128, 256], BF16)
    xk = [din("xk0", [128, KC, KVT[0]]), din("xk1", [128, KC, KVT[1]])]
    r2t_d = [din("r2t0", [128, 2, 2 * SEQ_NCC[0]], BF16), din("r2t1", [128, 2, 2 * SEQ_NCC[1]], BF16)]
    vb_d = [din("vb0", [128, NQT[0] * 16]), din("vb1", [128, NQT[1] * 16])]
    tm_d = [din("tm0", [128, E64[0]]), din("tm1", [128, E64[1]])]
    pT_d = [din("pT0", [128, 2, SEQ_OWN[0]]), din("pT1", [128, 2, SEQ_OWN[1]])]
    y_d = [nc.dram_tensor("y0", [SEQ_OWN[0], D], F32, kind="ExternalOutput"),
           nc.dram_tensor("y1", [SEQ_OWN[1], D], F32, kind="ExternalOutput")]

    Zs = [dscr("Zs0", [128, 128, 2048]), dscr("Zs1", [64, 128, 2048])]
    XTs = [dscr("XTs%d" % s, [128, 16, SEQ_NCC[s] * 128]) for s in range(2)]
    Ks = [dscr("Ks%d" % s, [128, 8, KVT[s]]) for s in range(2)]
    Vs = [dscr("Vs%d" % s, [128, KVT[s] // 128, 1024]) for s in range(2)]
    Qs = [dscr("Qs%d" % s, [128, 8, E64[s]]) for s in range(2)]
    Gs = [dscr("Gs%d" % s, [128, 16, 2, E64[s]]) for s in range(2)]
    As = [dscr("As%d" % s, [128, 8, E64[s]]) for s in range(2)]
    X1b = [dscr("X1b%d" % s, [128, KC, E64[s]]) for s in range(2)]
    X1f = [dscr("X1f%d" % s, [128, KC, E64[s]], F32) for s in range(2)]
    Hs = dscr("Hs", [128, 44, 3072])
    Wp = dscr("Wp", [16, 128, 16, 128])
    nwb = dscr("nwb", [16, 128, 8, 128])
    woutb = dscr("woutb", [16, 128, 16, 128])
    fdnb = dscr("fdnb", [16, 128, 44, 128])
    pgb = dscr("pgb", [16, 128, 16, 128])
    ppb = dscr("ppb", [16, 128, 2, 128])

    ps = [nc.alloc_psum_tensor("psb%d" % i, [128, 512], F32) for i in range(8)]
    dram_names = set()
    for t_ in (Zs + XTs + Ks + Vs + Qs + Gs + As + X1b + X1f + [Hs, Wp, nwb, woutb, fdnb, pgb, ppb] + y_d):
        dram_names.add(t_.name)

    def K(t, *i):
        return (t.name,) + tuple(i)

    def dma(out, in_, reads, writes, eng="sp"):
        reads = [k for k in reads if k[0] not in dram_names]
        writes = [k for k in writes if k[0] not in dram_names]
        P.add(eng, lambda e, o=out, i=in_: e.dma_start(out=o, in_=i), reads=reads, writes=writes, dma=True)

    vec_sb = A.alloc([128, 32 + 64 + 352], F32)
    ccs_sb = A.alloc([128, 256], F32)
    idf_sb = A.alloc([128, 128], F32)
    cbf_sb = A.alloc([128, 256], BF16)
    eps_sb = A.alloc([128, 1], F32)
    dummy = A.alloc([128, 8], F32)
    for (sb, dr) in ((vec_sb, vecs), (ccs_sb, ccs), (idf_sb, ident_f_d), (cbf_sb, cbf_d)):
        dma(sb[:], dr[:], [], [K(sb)])
    P.add("dve", lambda e: e.memset(eps_sb[:], 1e-5), writes=[K(eps_sb)])
    ident_b = cbf_sb[:, 0:128]
    ones_b = cbf_sb[:, 128:256]
    gb_sb = vec_sb[:, 0:32]
    ln_g = [vec_sb[:, 32:48], vec_sb[:, 64:80]]
    ln_b = [vec_sb[:, 48:64], vec_sb[:, 80:96]]
    CW0 = 96
    CB0 = 96 + 264
    pstg = [A.alloc([128, 2048], F32) for _ in range(1)]
    pstb = [A.alloc([128, 2048], BF16) for _ in range(2)]
    base_mark = A.mark()

    rot = {"i": 0}

    def alt(engs=("dve", "act")):
        rot["i"] += 1
        return engs[rot["i"] % len(engs)]

    def copy_op(eng, out, in_, reads, writes, scale=None):
        if eng == "act":
            if scale is None:
                P.add("act", lambda e: e.copy(out=out, in_=in_), reads=reads, writes=writes)
            else:
                P.add("act", lambda e: e.mul(out=out, in_=in_, mul=scale), reads=reads, writes=writes)
        else:
            if scale is None:
                P.add(eng, lambda e: e.tensor_copy(out=out, in_=in_), reads=reads, writes=writes)
            else:
                P.add(eng, lambda e: e.tensor_scalar(out=out, in0=in_, scalar1=scale, scalar2=None, op0=ALU.mult),
                      reads=reads, writes=writes)

    deferred = []

    def run_deferred(n=None):
        k = len(deferred) if n is None else min(n, len(deferred))
        for _ in range(k):
            deferred.pop(0)()

    prep_pieces = []
    for (src_, dst_) in ((nw, nwb), (wout, woutb), (pg, pgb), (fdn, fdnb)):
        for m_ in range(16):
            prep_pieces.append((src_, dst_, m_))

    def prep_dma(n):
        for _ in range(n):
            if not prep_pieces:
                return
            src_, dst_, m_ = prep_pieces.pop(0)
            dma(dst_[m_], src_[m_], [], [K(dst_)], eng="pool")

    def phase_prep():
        dma(ppb[:], pp[:], [], [K(ppb)], eng="pool")

    def prep_g(g):
        if True:
            s = 0
            dma(pstg[s][:], fw[:, g, :], [], [K(pstg[s])])
            for ri in range(2):
                s2 = ri
                for q in range(4):
                    b = ps[4 + q]
                    P.add("pe", lambda e, b=b, s=s, ri=ri, q=q: e.matmul(
                        b[:], lhsT=ccs_sb[:, ri * 128:(ri + 1) * 128], rhs=pstg[s][:, q * 512:(q + 1) * 512],
                        start=True, stop=True), reads=[K(pstg[s]), K(ccs_sb)], writes=[K(b)])
                    copy_op(alt(), pstb[s2][:, q * 512:(q + 1) * 512], b[:], [K(b)], [K(pstb[s2], q)])
                for m in range(16):
                    dma(Wp[m, :, g * 2 + ri, :], pstb[s2][:, m * 128:(m + 1) * 128], [K(pstb[s2], m // 4)], [K(Wp)], eng="act")

    def phase_a():
        wfb = A.alloc([128, KC, 1024], BF16)
        dma(wfb[:], win_f[:], [], [K(wfb)], eng="pool")
        NS = 3
        xb = [A.alloc([128, KC, 128], BF16) for _ in range(NS)]
        tt = [A.alloc([128, 256], BF16) for _ in range(NS)]
        fa = [A.alloc([128, 1024], BF16) for _ in range(2)]
        zt = [A.alloc([128, 2048], BF16) for _ in range(2)]

        def load(t):
            s = t % NS
            dma(xb[s][:], xfa[t], [], [K(xb[s])], eng="pool")
            dma(tt[s][:], tfa[t], [], [K(tt[s])])

        def proj(t):
            s, sf = t % NS, t % 2
            for half in range(2):
                b = ps[sf * 2 + half]
                for kc in range(KC):
                    P.add("pe", lambda e, b=b, s=s, kc=kc, half=half: e.matmul(
                        b[:], lhsT=xb[s][:, kc, :], rhs=wfb[:, kc, half * 512:(half + 1) * 512],
                        start=(kc == 0), stop=(kc == KC - 1)), reads=[K(xb[s]), K(wfb)], writes=[K(b)])
                copy_op(("act", "dve")[half], fa[sf][:, half * 512:(half + 1) * 512], b[:], [K(b)], [K(fa[sf], half)])

        def stage1(t):
            s, sf = t % NS, t % 2
            seq, a = (0, t) if t < 128 else (1, t - 128)
            for ri in range(2):
                for half in range(2):
                    b = ps[4 + ri * 2 + half]
                    P.add("pe", lambda e, b=b, s=s, sf=sf, ri=ri, half=half: e.matmul(
                        b[:], lhsT=tt[s][:, ri * 128:(ri + 1) * 128], rhs=fa[sf][:, half * 512:(half + 1) * 512],
                        start=True, stop=True), reads=[K(tt[s]), K(fa[sf], half)], writes=[K(b)])
                    o = ri * 1024 + half * 512
                    copy_op(("act", "dve")[(ri + half) % 2], zt[sf][:, o:o + 512], b[:], [K(b)], [K(zt[sf], ri * 2 + half)])
            dma(Zs[seq][a, :, :], zt[sf][:], [K(zt[sf], q) for q in range(4)], [K(Zs[seq])])

        load(0)
        load(1)
        for t in range(192):
            proj(t)
            if t >= 1:
                stage1(t - 1)
            if t + 2 < 192:
                load(t + 2)
            if t % 3 == 2:
                prep_dma(1)
            if t == 5:
                phase_prep()
        stage1(191)

    def phase_b(seq):
        Pn, ncc = SEQ_P[seq], SEQ_NCC[seq]
        N = 2 * ncc
        r2 = A.alloc([128, 2, N], BF16)
        dma(r2[:], r2t_d[seq][:], [], [K(r2)])
        xtt = A.alloc([128, 16, ncc, 128], BF16)
        DB = 4
        zd = [A.alloc([128, DB, 2048], BF16) for _ in range(3)]

        def load(i):
            dma(zd[i % 3][0:Pn, :, :], Zs[seq][:, i * DB:(i + 1) * DB, :], [K(Zs[seq])], [K(zd[i % 3])], eng=("sp", "pool")[i % 2])

        nld = 128 // DB
        load(0)
        load(1)
        for i in range(nld):
            s = i % 3
            for dd in range(DB):
                d = i * DB + dd
                b = ps[d % 8]
                for g in range(8):
                    P.add("pe", lambda e, b=b, s=s, g=g, dd=dd: e.matmul(
                        b[:, g * N:(g + 1) * N], lhsT=zd[s][0:Pn, dd, g * 128:(g + 1) * 128], rhs=r2[0:Pn, 0, :],
                        start=True, stop=False), reads=[K(zd[s]), K(r2)], writes=[K(b)])
                    P.add("pe", lambda e, b=b, s=s, g=g, dd=dd: e.matmul(
                        b[:, g * N:(g + 1) * N], lhsT=zd[s][0:Pn, dd, 1024 + g * 128:1024 + (g + 1) * 128], rhs=r2[0:Pn, 1, :],
                        start=False, stop=True), reads=[K(zd[s]), K(r2)], writes=[K(b)])
                copy_op(("dve", "act")[d % 2], xtt[:, :, :, d], b[:, 0:16 * ncc].rearrange("p (j c) -> p j c", c=ncc),
                        [K(b)], [K(xtt)])
            if i + 2 < nld:
                load(i + 2)
        for j in range(16):
            dma(XTs[seq][:, j, :], xtt[:, j, :, :].rearrange("p c d -> p (c d)"), [K(xtt)], [K(XTs[seq])], eng="act")

    def phase_c(seq):
        kvt, e64 = KVT[seq], E64[seq]
        src = A.alloc([128, KC, kvt], BF16)
        do_prep = (seq == 0)
        srck = [K(src, i) for i in range(kvt // 512)]
        NS = 3
        wb = [A.alloc([128, KC, 128], BF16) for _ in range(NS)]
        ost = [A.alloc([128, 512], BF16) for _ in range(6)]
        kv_tiles = [(o, min(512, kvt - o)) for o in range(0, kvt, 512)]
        q_tiles = [(KV_E0 + o, min(512, e64 - o)) for o in range(0, e64, 512)]
        jobs = [("k", h, 16 + h) for h in range(8)] + [("q", h, 8 + h) for h in range(8)] + \
               [("g", j, 32 + j) for j in range(32)]

        def load(ji):
            dma(wb[ji % NS][:], win_c[jobs[ji][2]], [], [K(wb[ji % NS])], eng="pool")

        vwb = A.alloc([128, KC, 512], BF16)
        nb = 0
        no = 0
        load(0)
        for i in range((kvt + 511) // 512):
            pw_ = min(512, kvt - i * 512)
            dma(src[:, :, i * 512:i * 512 + pw_], xk[seq][:, :, i * 512:i * 512 + pw_], [], [K(src, i)], eng="pool")
            if i == 0:
                load(1)
        for ji, (kind, idx, _c) in enumerate(jobs):
            s = ji % NS
            if ji + 2 < len(jobs):
                load(ji + 2)
            for (o, W) in (kv_tiles if kind == "k" else q_tiles):
                b = ps[nb % 6]
                nb += 1
                sk = [K(src, i_) for i_ in range(o // 512, (o + W - 1) // 512 + 1)]
                for kc in range(KC):
                    P.add("pe", lambda e, b=b, s=s, kc=kc, o=o, W=W: e.matmul(
                        b[:, 0:W], lhsT=wb[s][:, kc, :], rhs=src[:, kc, o:o + W], start=(kc == 0), stop=(kc == KC - 1)),
                        reads=[K(wb[s])] + sk, writes=[K(b)])
                os_ = ost[no % 6]
                no += 1
                if kind == "k":
                    copy_op(alt(), os_[:, 0:W], b[:, 0:W], [K(b)], [K(os_)])
                    dma(Ks[seq][:, idx, o:o + W], os_[:, 0:W], [K(os_)], [K(Ks[seq])])
                elif kind == "q":
                    copy_op(alt(), os_[:, 0:W], b[:, 0:W], [K(b)], [K(os_)], scale=128.0 ** -0.5)
                    dma(Qs[seq][:, idx, o - KV_E0:o - KV_E0 + W], os_[:, 0:W], [K(os_)], [K(Qs[seq])])
                else:
                    P.add("act", lambda e, b=b, os_=os_, W=W, idx=idx: e.activation(
                        out=os_[:, 0:W], in_=b[:, 0:W], func=AF.Sigmoid, bias=gb_sb[:, idx:idx + 1]),
                        reads=[K(b), K(vec_sb)], writes=[K(os_)])
                    dma(Gs[seq][:, idx % 16, idx // 16, o - KV_E0:o - KV_E0 + W], os_[:, 0:W], [K(os_)], [K(Gs[seq])])
        if do_prep:
            prep_dma(1000)
        for half in range(2):
            dma(vwb[:], win_v[half], [], [K(vwb)], eng="pool")
            for sub in range(kvt // 128):
                b = ps[nb % 6]
                nb += 1
                for kc in range(KC):
                    P.add("pe", lambda e, b=b, kc=kc, sub=sub: e.matmul(
                        b[:], lhsT=src[:, kc, sub * 128:(sub + 1) * 128], rhs=vwb[:, kc, :],
                        start=(kc == 0), stop=(kc == KC - 1)), reads=[K(src, sub // 4), K(vwb)], writes=[K(b)])
                os_ = ost[no % 6]
                no += 1
                copy_op(alt(), os_[:], b[:], [K(b)], [K(os_)])
                dma(Vs[seq][:, sub, half * 512:(half + 1) * 512], os_[:], [K(os_)], [K(Vs[seq])])

    def phase_d(seq):
        nq = NQT[seq]
        bt = A.alloc([128, 8, 1024], BF16)
        dma(bt[:], bias_t[:], [], [K(bt)])
        vb = A.alloc([128, nq * 16], F32)
        dma(vb[:], vb_d[seq][:], [], [K(vb)])
        qt = [A.alloc([128, 8, 128], BF16) for _ in range(2)]
        RK = 10
        kt = A.alloc([128, 8, RK * 128], BF16)
        vt = A.alloc([128, RK, 1024], BF16)
        npairs = KVT[seq] // 128
        pt = [A.alloc([128, 1024], BF16) for _ in range(2)]
        rd = A.alloc([128, 1024], F32)
        osb = A.alloc([128, 1024], F32)
        at = [A.alloc([128, 8, 128], BF16) for _ in range(2)]
        def load(i):
            s = i % 2
            dma(qt[s][:], Qs[seq][:, :, i * 128:(i + 1) * 128], [K(Qs[seq])], [K(qt[s])])

        def load_pair(p):
            if p >= npairs:
                return
            sl = p % RK
            dma(kt[:, :, sl * 128:(sl + 1) * 128], Ks[seq][:, :, p * 128:(p + 1) * 128], [], [K(kt, sl)])
            dma(vt[:, sl, :], Vs[seq][:, p, :], [], [K(vt, sl)])

        steps = []
        for i in range(nq):
            if i < 4:
                kps = list(range(1, 8))
            elif i >= nq - 3:
                kps = list(range(0, 6))
            else:
                kps = list(range(1, 6))
            for kp in kps:
                steps.append((i, kp, kps[0], kps[-1]))

        def score_exp(n):
            i, kp, k0, k1 = steps[n]
            s = i % 2
            sc = (ps[(n % 2) * 2], ps[(n % 2) * 2 + 1])
            p_ = pt[n % 2]
            for h in range(8):
                cs = slice(h * 64, (h + 1) * 64)
                for qb in range(2):
                    P.add("pe", lambda e, b=sc[qb], cs=cs, s=s, h=h, sl=(i + kp - 1) % RK, qb=qb: e.matmul(
                        b[:, cs], lhsT=kt[:, h, sl * 128:(sl + 1) * 128], rhs=qt[s][:, h, qb * 64:(qb + 1) * 64],
                        start=True, stop=False), reads=[K(kt, (i + kp - 1) % RK), K(qt[s])], writes=[K(sc[qb])])
                for qb in range(2):
                    P.add("pe", lambda e, b=sc[qb], cs=cs, h=h, kp=kp, qb=qb: e.matmul(
                        b[:, cs], lhsT=ident_b, rhs=bt[:, kp, h * 128 + qb * 64:h * 128 + qb * 64 + 64],
                        start=False, stop=True), reads=[K(bt), K(cbf_sb)], writes=[K(sc[qb])])
            for qb in range(2):
                col = (i * 8 + kp) * 2 + qb
                iv = sc[qb][:].rearrange("p (h q) -> p h q", h=8)
                ov = p_[:].rearrange("p (h b q) -> p h b q", h=8, b=2)[:, :, qb, :]
                P.add("act", lambda e, iv=iv, ov=ov, col=col: e.activation(
                    out=ov, in_=iv, func=AF.Exp, bias=vb[:, col:col + 1]),
                    reads=[K(sc[qb]), K(vb)], writes=[K(p_, qb)])

        def pv(n):
            i, kp, k0, k1 = steps[n]
            s = i % 2
            p_ = pt[n % 2]
            prk = [K(p_, 0), K(p_, 1)]
            for h in range(8):
                b = ps[4 + h // 4]
                cs = slice((h % 4) * 128, (h % 4 + 1) * 128)
                P.add("pe", lambda e, b=b, cs=cs, sl=(i + kp - 1) % RK, h=h, kp=kp, p_=p_, k0=k0, k1=k1: e.matmul(
                    b[:, cs], lhsT=vt[:, sl, h * 128:(h + 1) * 128], rhs=p_[:, h * 128:(h + 1) * 128],
                    start=(kp == k0 and h % 4 == 0), stop=(kp == k1)), reads=[K(vt, (i + kp - 1) % RK)] + prk, writes=[K(b)])
            for bk in range(2):
                b = ps[6 + bk]
                P.add("pe", lambda e, b=b, bk=bk, kp=kp, p_=p_, k0=k0, k1=k1: e.matmul(
                    b[:], lhsT=ones_b, rhs=p_[:, bk * 512:(bk + 1) * 512], start=(kp == k0), stop=(kp == k1)),
                    reads=prk + [K(cbf_sb)], writes=[K(b)])
            if kp == k1:
                for bk in range(2):
                    eng_ = ("dve", "act")[bk]
                    copy_op(eng_, osb[:, bk * 512:(bk + 1) * 512], ps[4 + bk][:], [K(ps[4 + bk])], [K(osb, bk)])
                    if eng_ == "dve":
                        P.add("dve", lambda e, bk=bk: e.reciprocal(out=rd[:, bk * 512:(bk + 1) * 512], in_=ps[6 + bk][:]),
                              reads=[K(ps[6 + bk])], writes=[K(rd, bk)])
                    else:
                        copy_op("act", rd[:, bk * 512:(bk + 1) * 512], ps[6 + bk][:], [K(ps[6 + bk])], [K(rd, bk)])
                        P.add("dve", lambda e, bk=bk: e.reciprocal(out=rd[:, bk * 512:(bk + 1) * 512], in_=rd[:, bk * 512:(bk + 1) * 512]),
                              reads=[K(rd, bk)], writes=[K(rd, bk)])
                for bk in range(2):
                    P.add(("dve", "pool")[bk], lambda e, bk=bk, s=s: e.tensor_tensor(
                        out=at[s][:, bk * 4:(bk + 1) * 4, :].rearrange("p h q -> p (h q)"), in0=osb[:, bk * 512:(bk + 1) * 512],
                        in1=rd[:, bk * 512:(bk + 1) * 512], op=ALU.mult), reads=[K(osb, bk), K(rd, bk)], writes=[K(at[s], bk)])
                dma(As[seq][:, :, i * 128:(i + 1) * 128], at[s][:], [K(at[s], 0), K(at[s], 1)], [K(As[seq])])

        load(0)
        for p_i in range(8):
            load_pair(p_i)
        if nq > 1:
            load(1)
        load_pair(8)
        load_pair(9)
        score_exp(0)
        for n in range(len(steps)):
            i, kp, k0, k1 = steps[n]
            if n + 1 < len(steps):
                score_exp(n + 1)
            pv(n)
            if kp == k1:
                if i + 2 < nq:
                    load(i + 2)
                if i >= 1:
                    load_pair(i - 1 + RK)

    def ln_chunk_stats(pre, kc, W, scr_b, scr_q, sq_eng="act"):
        P.add("act", lambda e: e.copy(out=scr_b[:, kc, 0:W], in_=pre[:, kc, 0:W]), reads=[K(pre, kc)], writes=[K(scr_b, kc)])
        if sq_eng == "act":
            P.add("act", lambda e: e.activation(out=scr_q[:, kc, 0:W], in_=pre[:, kc, 0:W], func=AF.Square), reads=[K(pre, kc)], writes=[K(scr_q, kc)])
        else:
            P.add("dve", lambda e: e.tensor_tensor(out=scr_q[:, kc, 0:W], in0=pre[:, kc, 0:W], in1=pre[:, kc, 0:W], op=ALU.mult),
                  reads=[K(pre, kc)], writes=[K(scr_q, kc)])

    def ln_thunks(pre, W, which, scr_b, scr_q, st, out_of, fin, sq_eng="act", mul_eng="pool", split=False):
        mean, msq, var, rstd = st
        pk = [K(pre, kc) for kc in range(KC)]
        th = []

        sbk = [K(scr_b, kc) for kc in range(KC)]
        sqk = [K(scr_q, kc) for kc in range(KC)]

        def stats2():
            for kc in range(KC):
                P.add("pe", lambda e, kc=kc: e.matmul(ps[6][:, 0:W], lhsT=ones_b, rhs=scr_b[:, kc, 0:W],
                                                      start=(kc == 0), stop=(kc == KC - 1)),
                      reads=[K(scr_b, kc), K(cbf_sb)], writes=[K(ps[6])])
            for kc in range(KC):
                P.add("pe", lambda e, kc=kc: e.matmul(ps[7][:, 0:W], lhsT=ones_b, rhs=scr_q[:, kc, 0:W],
                                                      start=(kc == 0), stop=(kc == KC - 1)),
                      reads=[K(scr_q, kc), K(cbf_sb)], writes=[K(ps[7])])
            P.add("dve", lambda e: e.tensor_scalar(out=mean[:, 0:W], in0=ps[6][:, 0:W], scalar1=1.0 / D, scalar2=None, op0=ALU.mult),
                  reads=[K(ps[6])], writes=[K(mean)])
            P.add("dve", lambda e: e.tensor_tensor(out=msq[:, 0:W], in0=mean[:, 0:W], in1=mean[:, 0:W], op=ALU.mult),
                  reads=[K(mean)], writes=[K(msq)])
            P.add("dve", lambda e: e.scalar_tensor_tensor(out=var[:, 0:W], in0=ps[7][:, 0:W], scalar=1.0 / D, in1=msq[:, 0:W],
                                                          op0=ALU.mult, op1=ALU.subtract),
                  reads=[K(ps[7]), K(msq)], writes=[K(var)])
            P.add("act", lambda e: e.activation(out=var[:, 0:W], in_=var[:, 0:W], func=AF.Sqrt, bias=eps_sb[:, 0:1]),
                  reads=[K(var), K(eps_sb)], writes=[K(var)])
            P.add("dve", lambda e: e.reciprocal(out=rstd[:, 0:W], in_=var[:, 0:W]), reads=[K(var)], writes=[K(rstd)])

        th.append(stats2)

        def norm(kc, do_fin=True):
            o = out_of(kc)
            ok = o[1]
            P.add("dve", lambda e: e.tensor_tensor(out=o[0], in0=pre[:, kc, 0:W], in1=mean[:, 0:W], op=ALU.subtract),
                  reads=[K(pre, kc), K(mean), K(rstd)], writes=[ok])
            P.add(mul_eng, lambda e: e.tensor_tensor(out=o[0], in0=o[0], in1=rstd[:, 0:W], op=ALU.mult),
                  reads=[ok, K(rstd)], writes=[ok])
            P.add("act", lambda e: e.activation(out=o[0], in_=o[0], func=AF.Identity,
                                                bias=ln_b[which][:, kc:kc + 1], scale=ln_g[which][:, kc:kc + 1]),
                  reads=[ok, K(vec_sb)], writes=[ok])
            if do_fin:
                fin(kc)

        if not split:
            for kc in range(KC):
                th.append(lambda kc=kc: norm(kc))
        else:
            for kc in range(KC):
                th.append(lambda kc=kc: norm(kc, False))
                if kc >= 1:
                    th.append(lambda kc=kc: fin(kc - 1))
            th.append(lambda: fin(KC - 1))
        return th

    def phase_e(seq):
        e64 = E64[seq]
        nt_ = (e64 + 511) // 512
        wbase = ((e64 // nt_) + 63) // 64 * 64
        tiles = []
        o_ = 0
        while o_ < e64:
            tiles.append((o_, min(wbase, e64 - o_)))
            o_ += wbase
        xt = A.alloc([128, 16, 512], BF16)
        att = A.alloc([128, 8, 512], BF16)
        gt = [A.alloc([128, 2, 512], BF16) for _ in range(4)]
        mg = A.alloc([128, 16, 512], BF16)
        pre = A.alloc([128, 16, 512], F32)
        xres = [A.alloc([128, 512], F32) for _ in range(4)]
        scr_b = A.alloc([128, 16, 512], BF16)
        scr_q = A.alloc([128, 16, 512], BF16)
        st = [A.alloc([128, 512], F32) for _ in range(3)]
        st = [st[0], st[2], st[1], st[2]]
        t1 = [A.alloc([128, 512], F32) for _ in range(3)]
        t2 = [A.alloc([128, 512], F32) for _ in range(3)]
        tmk = [A.alloc([128, 512], F32) for _ in range(2)]
        x1o = [A.alloc([128, 512], F32) for _ in range(3)]
        x1ob = [A.alloc([128, 512], BF16) for _ in range(3)]
        wa = [A.alloc([128, 16, 128], BF16) for _ in range(4)]
        wn = [A.alloc([128, 8, 128], BF16) for _ in range(4)]
        wo = wa
        cnt = {"n": 0, "o": 0, "w": 0}

        for ti, (o, W) in enumerate(tiles):
            dma(xt[:, :, 0:W], XTs[seq][:, :, 64 + o:64 + o + W], [K(XTs[seq])], [K(xt)])
            dma(att[:, :, 0:W], As[seq][:, :, o:o + W], [K(As[seq])], [K(att)])

            def loadw(m, o=o, W=W):
                s = m % 4
                dma(wa[s][:], Wp[m], [K(Wp)], [K(wa[s])])
                dma(wn[s][:], nwb[m], [K(nwb)], [K(wn[s])])
                dma(gt[s][:, :, 0:W], Gs[seq][:, m, :, o:o + W], [K(Gs[seq])], [K(gt[s])])

            loadw(0)
            loadw(1)
            for m in range(16):
                s = m % 4
                n = cnt["n"]
                cnt["n"] += 1
                k2 = n % 3
                ba, bb = ps[k2 * 2], ps[k2 * 2 + 1]
                for j in range(16):
                    P.add("pe", lambda e, ba=ba, s=s, j=j, W=W: e.matmul(ba[:, 0:W], lhsT=wa[s][:, j, :], rhs=xt[:, j, 0:W],
                                                                         start=(j == 0), stop=(j == 15)),
                          reads=[K(wa[s]), K(xt)], writes=[K(ba)])
                for j in range(8):
                    P.add("pe", lambda e, bb=bb, s=s, j=j, W=W: e.matmul(bb[:, 0:W], lhsT=wn[s][:, j, :], rhs=att[:, j, 0:W],
                                                                         start=(j == 0), stop=(j == 7)),
                          reads=[K(wn[s]), K(att)], writes=[K(bb)])
                P.add("dve", lambda e, ba=ba, s=s, k2=k2, W=W: e.tensor_tensor(out=t1[k2][:, 0:W], in0=ba[:, 0:W], in1=gt[s][:, 0, 0:W], op=ALU.mult),
                      reads=[K(ba), K(gt[s])], writes=[K(t1[k2])])
                P.add("dve", lambda e, bb=bb, s=s, k2=k2, W=W: e.tensor_tensor(out=t2[k2][:, 0:W], in0=bb[:, 0:W], in1=gt[s][:, 1, 0:W], op=ALU.mult),
                      reads=[K(bb), K(gt[s])], writes=[K(t2[k2])])
                P.add(("pool", "dve")[m % 2], lambda e, k2=k2, m=m, W=W: e.tensor_tensor(out=mg[:, m, 0:W], in0=t1[k2][:, 0:W], in1=t2[k2][:, 0:W], op=ALU.add),
                      reads=[K(t1[k2]), K(t2[k2])], writes=[K(mg, m)])
                if m + 2 < 16:
                    loadw(m + 2)
                run_deferred(2 if m == 0 else 1)
            run_deferred()
            mgk = [K(mg, m) for m in range(16)]

            def loado(m, o=o, W=W):
                s = m % 4
                dma(wo[s][:], woutb[m], [K(woutb)], [K(wo[s])])
                dma(xres[s][:, 0:W], xk[seq][:, m, KV_E0 + o:KV_E0 + o + W], [], [K(xres[s])])

            loado(0)
            loado(1)
            dma(tmk[ti % 2][:, 0:W], tm_d[seq][:, o:o + W], [], [K(tmk[ti % 2])])
            for m in range(16):
                s = m % 4
                b = ps[4 + m % 4]
                for j in range(16):
                    P.add("pe", lambda e, b=b, s=s, j=j, W=W: e.matmul(b[:, 0:W], lhsT=wo[s][:, j, :], rhs=mg[:, j, 0:W],
                                                                       start=(j == 0), stop=(j == 15)),
                          reads=[K(wo[s]), K(mg, j)], writes=[K(b)])
                P.add("dve", lambda e, b=b, m=m, s=s, W=W: e.scalar_tensor_tensor(out=pre[:, m, 0:W], in0=xres[s][:, 0:W], scalar=ALPHA, in1=b[:, 0:W],
                                                                                  op0=ALU.mult, op1=ALU.add),
                      reads=[K(b), K(xres[s])], writes=[K(pre, m)])
                ln_chunk_stats(pre, m, W, scr_b, scr_q, "act")
                if m + 2 < 16:
                    loado(m + 2)

            def out_of(kc, W=W):
                i = cnt["o"] % 3
                return (x1o[i][:, 0:W], K(x1o[i]), i)

            def fin(kc, o=o, W=W, ti=ti, tail=(seq == 1 and ti == len(tiles) - 1)):
                mk_eng = "pool" if (tail and kc % 2 == 0) else "dve"
                i = cnt["o"] % 3
                cnt["o"] += 1
                dma(X1f[seq][:, kc, o:o + W], x1o[i][:, 0:W], [K(x1o[i])], [K(X1f[seq])], eng="act")
                P.add(mk_eng, lambda e: e.tensor_tensor(out=x1ob[i][:, 0:W], in0=x1o[i][:, 0:W], in1=tmk[ti % 2][:, 0:W], op=ALU.mult),
                      reads=[K(x1o[i]), K(tmk[ti % 2])], writes=[K(x1ob[i])])
                dma(X1b[seq][:, kc, o:o + W], x1ob[i][:, 0:W], [K(x1ob[i])], [K(X1b[seq])], eng=("pool" if mk_eng == "pool" else "act"))

            last_ = (seq == 1 and ti == len(tiles) - 1)
            deferred.extend(ln_thunks(pre, W, 0, scr_b, scr_q, st, out_of, fin, mul_eng=("dve" if last_ else "pool")))
        if seq == 1:
            run_deferred()

    def phase_f():
        res = [A.alloc([128, KC, SEQ_OWN[s] + 2], BF16) for s in range(2)]
        for s in range(2):
            nres = SEQ_OWN[s] + 2
            for pi, po in enumerate(range(0, nres, 512)):
                pw = min(512, nres - po)
                dma(res[s][:, :, po:po + pw], X1b[s][:, :, 63 + po:63 + po + pw], [], [K(res[s], pi)],
                    eng=("sp", "act")[pi % 2])
        NS = 6
        wb = [A.alloc([128, KC, 128], BF16) for _ in range(NS)]
        ug = [A.alloc([128, 512], F32) for _ in range(2)]
        ul = [A.alloc([128, 512], F32) for _ in range(2)]
        gg = [A.alloc([128, 512], F32) for _ in range(2)]
        hh = [A.alloc([128, 512], BF16) for _ in range(3)]
        tiles = []
        for s in range(2):
            ntl = (SEQ_OWN[s] + 509) // 510
            wbase = (SEQ_OWN[s] + ntl - 1) // ntl
            o_ = 0
            while o_ < SEQ_OWN[s]:
                tiles.append((s, o_, min(wbase, SEQ_OWN[s] - o_)))
                o_ += wbase

        def load(j):
            for part in range(2):
                sw = (j * 2 + part) % NS
                dma(wb[sw][:], fup[part * 44 + j], [], [K(wb[sw])], eng="pool")

        nt = 0
        nh = 0
        load(0)
        load(1)
        for j in range(44):
            if j + 2 < 44:
                load(j + 2)
            wsl = [(j * 2) % NS, (j * 2 + 1) % NS]
            for (s, o, W) in tiles:
                k = nt % 2
                nt += 1
                bg, bl = ps[k * 2], ps[k * 2 + 1]
                for part, b in ((0, bg), (1, bl)):
                    for kc in range(KC):
                        P.add("pe", lambda e, b=b, sw=wsl[part], kc=kc, s=s, o=o, W=W: e.matmul(
                            b[:, 0:W + 2], lhsT=wb[sw][:, kc, :], rhs=res[s][:, kc, o:o + W + 2],
                            start=(kc == 0), stop=(kc == KC - 1)),
                            reads=[K(wb[wsl[part]])] + [K(res[s], pi_) for pi_ in range(o // 512, (o + W + 1) // 512 + 1)], writes=[K(b)])
                for part, b, u in ((0, bg, ug[k]), (1, bl, ul[k])):
                    jj = part * 44 + j
                    w0 = vec_sb[:, CW0 + jj * 3 + 0:CW0 + jj * 3 + 1]
                    w1 = vec_sb[:, CW0 + jj * 3 + 1:CW0 + jj * 3 + 2]
                    w2 = vec_sb[:, CW0 + jj * 3 + 2:CW0 + jj * 3 + 3]
                    cb = vec_sb[:, CB0 + jj:CB0 + jj + 1]
                    P.add("act", lambda e, b=b, u=u, w1=w1, cb=cb, W=W: e.activation(
                        out=u[:, 0:W], in_=b[:, 1:W + 1], func=AF.Identity, bias=cb, scale=w1),
                        reads=[K(b), K(vec_sb)], writes=[K(u)])
                    P.add("dve", lambda e, b=b, u=u, w0=w0, W=W: e.scalar_tensor_tensor(
                        out=u[:, 0:W], in0=b[:, 0:W], scalar=w0, in1=u[:, 0:W], op0=ALU.mult, op1=ALU.add),
                        reads=[K(b), K(u), K(vec_sb)], writes=[K(u)])
                    P.add("dve", lambda e, b=b, u=u, w2=w2, W=W: e.scalar_tensor_tensor(
                        out=u[:, 0:W], in0=b[:, 2:W + 2], scalar=w2, in1=u[:, 0:W], op0=ALU.mult, op1=ALU.add),
                        reads=[K(b), K(u), K(vec_sb)], writes=[K(u)])
                P.add("act", lambda e, k=k, W=W: e.activation(out=gg[k][:, 0:W], in_=ug[k][:, 0:W], func=AF.Gelu_apprx_tanh),
                      reads=[K(ug[k])], writes=[K(gg[k])])
                h_ = hh[nh % 3]
                nh += 1
                P.add("dve", lambda e, k=k, W=W, h_=h_: e.tensor_tensor(out=h_[:, 0:W], in0=gg[k][:, 0:W], in1=ul[k][:, 0:W], op=ALU.mult),
                      reads=[K(gg[k]), K(ul[k])], writes=[K(h_)])
                oo = OWN_OFF[s] + o
                dma(Hs[:, j, oo:oo + W], h_[:, 0:W], [K(h_)], [K(Hs)])

    def phase_g():
        ht = A.alloc([128, 44, 512], BF16)
        xb1 = A.alloc([128, 16, 512], BF16)
        pb = A.alloc([128, 2, 512], BF16)
        pre = A.alloc([128, 16, 512], F32)
        xres = [A.alloc([128, 512], F32) for _ in range(2)]
        scr_b = A.alloc([128, 16, 512], BF16)
        scr_q = A.alloc([128, 16, 512], BF16)
        st = [A.alloc([128, 512], F32) for _ in range(3)]
        st = [st[0], st[2], st[1], st[2]]
        sg = [A.alloc([128, 512], F32) for _ in range(2)]
        wd = [A.alloc([128, 44, 128], BF16) for _ in range(2)]
        wg = [A.alloc([128, 16, 128], BF16) for _ in range(2)]
        wp = [A.alloc([128, 2, 128], BF16) for _ in range(2)]
        yc = [A.alloc([128, 512], F32) for _ in range(3)]
        yo = [A.alloc([128, 512], F32) for _ in range(3)]
        W = 512
        cnt = {"n": 0, "o": 0, "y": 0}
        for s in range(2):
            for o in range(0, SEQ_OWN[s], 512):
                oo = OWN_OFF[s] + o
                def loadw(m, s=s, o=o):
                    k = m % 2
                    dma(wd[k][:], fdnb[m], [K(fdnb)], [K(wd[k])])
                    dma(wg[k][:], pgb[m], [K(pgb)], [K(wg[k])])
                    dma(wp[k][:], ppb[m], [K(ppb)], [K(wp[k])])
                    dma(xres[k][:], X1f[s][:, m, 64 + o:64 + o + W], [K(X1f[s])], [K(xres[k])])

                loadw(0)
                for q_ in range(4):
                    dma(ht[:, q_ * 11:(q_ + 1) * 11, :], Hs[:, q_ * 11:(q_ + 1) * 11, oo:oo + W], [K(Hs)], [K(ht, q_)])
                dma(xb1[:], X1b[s][:, :, 64 + o:64 + o + W], [K(X1b[s])], [K(xb1)])
                dma(pb[:], pT_d[s][:, :, o:o + W], [], [K(pb)], eng="pool")
                for m in range(16):
                    k = m % 2
                    if m + 1 < 16:
                        loadw(m + 1)
                    n = cnt["n"]
                    cnt["n"] += 1
                    k2 = n % 2
                    b0, b1, b2 = ps[k2 * 3], ps[k2 * 3 + 1], ps[k2 * 3 + 2]
                    for j in range(44):
                        P.add("pe", lambda e, b0=b0, k=k, j=j: e.matmul(b0[:], lhsT=wd[k][:, j, :], rhs=ht[:, j, :],
                                                                        start=(j == 0), stop=(j == 43)),
                              reads=[K(wd[k]), K(ht, j // 11)], writes=[K(b0)])
                    for j in range(16):
                        P.add("pe", lambda e, b1=b1, k=k, j=j: e.matmul(b1[:], lhsT=wg[k][:, j, :], rhs=xb1[:, j, :],
                                                                        start=(j == 0), stop=(j == 15)),
                              reads=[K(wg[k]), K(xb1)], writes=[K(b1)])
                    for j in range(2):
                        P.add("pe", lambda e, b2=b2, k=k, j=j: e.matmul(b2[:], lhsT=wp[k][:, j, :], rhs=pb[:, j, :],
                                                                        start=(j == 0), stop=(j == 1)),
                              reads=[K(wp[k]), K(pb)], writes=[K(b2)])
                    run_deferred(2)
                    P.add("act", lambda e, b1=b1, k2=k2: e.activation(out=sg[k2][:], in_=b1[:], func=AF.Sigmoid),
                          reads=[K(b1)], writes=[K(sg[k2])])
                    P.add("dve", lambda e, b2=b2, k2=k2: e.tensor_tensor(out=sg[k2][:], in0=b2[:], in1=sg[k2][:], op=ALU.mult),
                          reads=[K(b2), K(sg[k2])], writes=[K(sg[k2])])
                    P.add("dve", lambda e, b0=b0, k2=k2: e.tensor_tensor(out=sg[k2][:], in0=b0[:], in1=sg[k2][:], op=ALU.add),
                          reads=[K(b0), K(sg[k2])], writes=[K(sg[k2])])
                    P.add("dve", lambda e, k=k, k2=k2, m=m: e.scalar_tensor_tensor(out=pre[:, m, :], in0=xres[k][:], scalar=ALPHA, in1=sg[k2][:],
                                                                                  op0=ALU.mult, op1=ALU.add),
                          reads=[K(sg[k2]), K(xres[k])], writes=[K(pre, m)])
                    ln_chunk_stats(pre, m, W, scr_b, scr_q, "dve")

                run_deferred()

                def out_of(kc):
                    i = kc % 3
                    return (yc[i][:], K(yc[i]), i)

                def fin(kc, s=s, o=o, ev_eng=("dve" if (s == 1 and o + 512 >= SEQ_OWN[s]) else "act")):
                    i = kc % 3
                    b = ps[6 + kc % 2]
                    for sub in range(4):
                        P.add("pe", lambda e, b=b, sub=sub, i=i: e.matmul(
                            b[:, sub * 128:(sub + 1) * 128], lhsT=yc[i][:, sub * 128:(sub + 1) * 128], rhs=idf_sb[:],
                            start=True, stop=True), reads=[K(yc[i]), K(idf_sb)], writes=[K(b)])
                    y_ = yo[cnt["y"] % 3]
                    cnt["y"] += 1
                    copy_op(ev_eng, y_[:], b[:], [K(b)], [K(y_)])
                    for sub in range(4):
                        dma(y_d[s][o + sub * 128:o + (sub + 1) * 128, kc * 128:(kc + 1) * 128], y_[:, sub * 128:(sub + 1) * 128],
                            [K(y_)], [K(y_d[s])], eng=("sp" if ev_eng == "dve" else "act"))

                last_ = (s == 1 and o + 512 >= SEQ_OWN[s])
                deferred.extend(ln_thunks(pre, W, 1, scr_b, scr_q, st, out_of, fin, sq_eng="dve", mul_eng=("dve" if last_ else "pool"), split=True))
        run_deferred()

    def run_phase(fn, *a, barrier=True):
        A.reset(base_mark)
        fn(*a)
        if barrier:
            P.barrier()

    P.barrier()
    for g_ in range(8):
        prep_g(g_)
    run_phase(phase_a)
    def run_group(fn):
        A.reset(base_mark)
        A.start_group()
        for s in range(2):
            if s > 0:
                A.next_member()
            fn(s)
        A.end_group()
        P.barrier()

    run_group(phase_b)
    for s in range(2):
        run_phase(phase_c, s)
    run_group(phase_d)
    run_group(phase_e)
    run_phase(phase_f)
    run_phase(phase_g)

    import contextlib
    with contextlib.ExitStack() as es:
        sems = {}
        for e_ in Prog.ENGS:
            sems[("c", e_)] = es.enter_context(nc.semaphore("c_" + e_))
            for k in range(Prog.KPOOL):
                sems[(e_, k)] = es.enter_context(nc.semaphore("d_%s%d" % (e_, k)))
        P.finalize(nc, sems)
        block = es.enter_context(nc.Block())

        @block.sync
        def _(e):
            P.emit("sp", e)

        @block.tensor
        def _(e):
            P.emit("pe", e)

        @block.vector
        def _(e):
            P.emit("dve", e)

        @block.scalar
        def _(e):
            P.emit("act", e)

        @block.gpsimd
        def _(e):
            P.emit("pool", e)
    return nc


def _fm(w):
    k, n = w.shape
    return np.ascontiguousarray(w.reshape(k // 128, 128, n).transpose(1, 0, 2))


def _chunks(w, cw=128):
    k, n = w.shape
    return np.ascontiguousarray(w.reshape(k // 128, 128, n // cw, cw).transpose(2, 1, 0, 3))


def _vec(v):
    return np.ascontiguousarray(v.reshape(-1, 128).T)


def _host_prep(inp):
    f32 = np.float32
    xs = [np.asarray(inp["x_prompt"], f32)[0], np.asarray(inp["x_sample"], f32)[0]]
    ps_ = [np.asarray(inp["p_prompt"], f32)[0, 0], np.asarray(inp["p_sample"], f32)[0, 0]]
    shared = {}
    xa0 = xs[0].reshape(128, 128, KC, 128).transpose(1, 3, 2, 0)
    xa1 = xs[1].reshape(128, 64, KC, 128).transpose(1, 3, 2, 0)
    shared["xfa"] = np.ascontiguousarray(np.concatenate([xa0, xa1], 0))
    tf = np.zeros((192, 128, 256), np.float64)
    bb = np.arange(128)[:, None]
    dd = np.arange(128)[None, :]
    for a in range(128):
        s = a + 128 * bb
        th = 2 * np.pi * ((s * dd) % 16384) / 16384.0
        tf[a, :, :128] = np.cos(th) / 128.0
        tf[a, :, 128:] = -np.sin(th) / 128.0
    for a in range(64):
        s = a + 64 * bb
        th = 2 * np.pi * ((s * dd) % 8192) / 8192.0
        tf[128 + a, :, :128] = np.cos(th) / np.sqrt(8192.0)
        tf[128 + a, :, 128:] = -np.sin(th) / np.sqrt(8192.0)
    shared["tfa"] = tf.astype(NPBF)
    w_in = np.asarray(inp["w_in"], f32)[0]
    shared["win_f"] = _fm(w_in[:, 0:1024])
    shared["win_c"] = _chunks(w_in)
    shared["win_v"] = _chunks(w_in[:, 3072:4096], 512)
    shared["fw"] = _fm(np.asarray(inp["fourier_w"], f32)[0])
    shared["nw"] = _chunks(np.asarray(inp["natten_w"], f32)[0])
    shared["wout"] = _chunks(np.asarray(inp["w_out"], f32)[0])
    shared["fup"] = _chunks(np.asarray(inp["ffn_up"], f32)[0])
    shared["fdn"] = _chunks(np.asarray(inp["ffn_down"], f32)[0])
    shared["pg"] = _chunks(np.asarray(inp["ple_gate"], f32)[0])
    shared["pp"] = _chunks(np.asarray(inp["ple_proj"], f32)[0])
    cw = np.asarray(inp["ffn_conv"], f32)[0]
    cwl = cw.T.reshape(88, 128, 3).transpose(1, 0, 2).reshape(128, 264)
    vecs = np.concatenate([
        _vec(np.asarray(inp["gate_b"], f32)[0]),
        _vec(np.asarray(inp["ln1_g"], f32)[0]), _vec(np.asarray(inp["ln1_b"], f32)[0]),
        _vec(np.asarray(inp["ln2_g"], f32)[0]), _vec(np.asarray(inp["ln2_b"], f32)[0]),
        cwl, _vec(np.asarray(inp["ffn_conv_b"], f32)[0])], axis=1)
    shared["vecs"] = np.ascontiguousarray(vecs.astype(f32))
    ch = np.arange(128)
    ang = 2 * np.pi * ((ch[:, None] * ch[None, :]) % 128) / 128.0
    shared["ccs"] = np.concatenate([np.cos(ang), np.sin(ang)], 1).astype(f32) / np.float32(np.sqrt(128.0))
    rpb = np.asarray(inp["natten_rpb"], f32)[0]
    bt = np.full((8, 128, 8, 2, 64), NEG, f32)
    qc = np.arange(64)
    cs = np.clip(qc - 8, 0, 48)
    for kp in range(8):
        for a in range(2):
            for b in range(2):
                dr = 2 * kp + a - 7 - b
                if abs(dr) > 7:
                    continue
                for kc_ in range(64):
                    ok = (kc_ >= cs) & (kc_ < cs + 16)
                    dc = kc_ - qc
                    vals = rpb[:, dr + 7, np.clip(dc + 15, 0, 30)]
                    bt[kp, a * 64 + kc_, :, b, :] = np.where(ok[None, :], vals, NEG)
    shared["bias_t"] = np.ascontiguousarray(bt.reshape(8, 128, 1024).transpose(1, 0, 2)).astype(NPBF)
    shared["ident_f"] = np.eye(128, dtype=f32)
    shared["cbf"] = np.concatenate([np.eye(128), np.ones((128, 128))], 1).astype(NPBF)

    maps = []
    for c in range(NCORES):
        m = dict(shared)
        for s in range(2):
            S, own, rows, Pn, ncc = SEQ_S[s], SEQ_OWN[s], SEQ_ROWS[s], SEQ_P[s], SEQ_NCC[s]
            t0 = c * own
            lo = t0 - 384
            hi = lo + KVT[s]
            xk = np.zeros((KVT[s], D), f32)
            a0, a1 = max(lo, 0), min(hi, S)
            xk[a0 - lo:a1 - lo] = xs[s][a0:a1]
            m["xk%d" % s] = np.ascontiguousarray(xk.T.reshape(KC, 128, KVT[s]).transpose(1, 0, 2))
            c0 = t0 // 128
            cc = np.arange(c0 - 1, c0 - 1 + ncc)
            aa = np.arange(Pn)
            ang = 2 * np.pi * ((aa[:, None] * cc[None, :]) % Pn) / float(Pn)
            r = np.zeros((128, 2, 2 * ncc), np.float64)
            r[:Pn, 0, :ncc] = np.cos(ang)
            r[:Pn, 0, ncc:] = -np.sin(ang)
            r[:Pn, 1, :ncc] = np.sin(ang)
            r[:Pn, 1, ncc:] = np.cos(ang)
            m["r2t%d" % s] = r.astype(NPBF)
            nq = NQT[s]
            r0 = t0 // 64
            vb = np.zeros((128, nq * 16), f32)
            for i in range(nq):
                for kp in range(8):
                    for b in range(2):
                        rq = r0 - 1 + 2 * i + b
                        col = (i * 8 + kp) * 2 + b
                        for a in range(2):
                            kr = r0 - 8 + 2 * (i + kp) + a
                            if rq < 0 or rq >= rows:
                                ok = True
                            else:
                                rs = min(max(rq - 4, 0), rows - 8)
                                ok = (rs <= kr < rs + 8)
                            vb[a * 64:(a + 1) * 64, col] = 0.0 if ok else NEG
            m["vb%d" % s] = vb
            tok = np.arange(t0 - 64, t0 - 64 + E64[s])
            tm = ((tok >= 0) & (tok < S)).astype(f32)
            m["tm%d" % s] = np.ascontiguousarray(np.broadcast_to(tm[None, :], (128, E64[s])))
            pt = ps_[s][t0:t0 + own]
            m["pT%d" % s] = np.ascontiguousarray(pt.T.reshape(2, 128, own).transpose(1, 0, 2))
        maps.append(m)
    return maps


_NC_CACHE = {}


def kernel(**inputs):
    maps = _host_prep(inputs)
    if "nc" not in _NC_CACHE:
        _NC_CACHE["nc"] = build_program()
    nc = _NC_CACHE["nc"]
    res = run_bass_kernel_spmd(nc, maps, core_ids=list(range(NCORES)))
    y0 = np.concatenate([np.asarray(res.results[c]["y0"], np.float32) for c in range(NCORES)], 0)[None]
    y1 = np.concatenate([np.asarray(res.results[c]["y1"], np.float32) for c in range(NCORES)], 0)[None]
    return (y0, y1)
```

```python
import numpy as np
import ml_dtypes
import concourse.bass as bass
import concourse.mybir as mybir
from concourse.bass_utils import run_bass_kernel_spmd

F32 = mybir.dt.float32
BF16 = mybir.dt.bfloat16
AF = mybir.ActivationFunctionType
ALU = mybir.AluOpType
NPBF = ml_dtypes.bfloat16

NCORES = 8
D = 2048
KC = 16
NEG = -30000.0
ALPHA = 2.0 ** 0.25
SEQ_S = [16384, 8192]
SEQ_OWN = [2048, 1024]
SEQ_ROWS = [256, 128]
SEQ_P = [128, 64]
SEQ_NCC = [18, 10]
E64 = [o + 128 for o in SEQ_OWN]
KVT = [o + 640 for o in SEQ_OWN]
KV_E0 = 320
NQT = [e // 128 for e in E64]
OWN_OFF = [0, 2048]
SB_BASE = 16640
SB_LIMIT = 229000


class Op:
    __slots__ = ("eng", "fn", "deps", "dma", "sig", "val", "sem", "idx")


class Prog:
    ENGS = ("pe", "act", "dve", "pool", "sp")
    KPOOL = 8

    def __init__(self):
        self.ops = []
        self.lastw = {}
        self.readers = {}
        self.eng_last = {}
        self.pending = {}
        self.barrier_set = set()
        self.dma_cnt = {e: 0 for e in self.ENGS}
        self.pool_last = {}

    def add(self, eng, fn, reads=(), writes=(), dma=False):
        idx = len(self.ops)
        deps = set()
        for k in reads:
            w = self.lastw.get(k)
            if w is not None:
                deps.add(w)
        for k in writes:
            w = self.lastw.get(k)
            if w is not None:
                deps.add(w)
            for r in self.readers.get(k, ()):
                deps.add(r)
        if self.pending.get(eng):
            deps |= self.barrier_set
            self.pending[eng] = False
        op = Op()
        op.eng, op.fn, op.dma, op.sig, op.val, op.sem, op.idx = eng, fn, dma, False, 0, None, idx
        if dma:
            n = self.dma_cnt[eng]
            self.dma_cnt[eng] = n + 1
            slot = (eng, n % self.KPOOL)
            prev = self.pool_last.get(slot)
            if prev is not None:
                deps.add(prev)
            self.pool_last[slot] = idx
            op.sem = slot
            op.val = 16 * (n // self.KPOOL + 1)
            op.sig = True
        op.deps = deps
        self.ops.append(op)
        for k in reads:
            self.readers.setdefault(k, []).append(idx)
        for k in writes:
            self.lastw[k] = idx
            self.readers[k] = []
        if not dma:
            self.eng_last[eng] = idx
        return idx

    def barrier(self):
        s = set(self.eng_last.values()) | set(self.pool_last.values())
        self.barrier_set = s
        for e in self.ENGS:
            self.pending[e] = True

    def finalize(self, nc, sems):
        self.barrier()
        self.add("sp", None)
        for op in self.ops:
            for d in op.deps:
                dop = self.ops[d]
                if dop.dma:
                    continue
                if dop.eng == "pe" and op.eng == "pe" and not op.dma:
                    continue
                dop.sig = True
        cnt = {e: 0 for e in self.ENGS}
        for op in self.ops:
            if not op.dma and op.sig:
                cnt[op.eng] += 1
                op.val = cnt[op.eng]
                op.sem = ("c", op.eng)
        self.by_eng = {e: [op for op in self.ops if op.eng == e] for e in self.ENGS}
        self.sems = sems

    def emit(self, eng, e):
        seen = {}
        for op in self.by_eng[eng]:
            waits = {}
            for d in op.deps:
                dop = self.ops[d]
                if (not dop.dma) and dop.eng == "pe" and eng == "pe" and not op.dma:
                    continue
                if dop.val > waits.get(dop.sem, 0):
                    waits[dop.sem] = dop.val
            for s, v in waits.items():
                if seen.get(s, 0) >= v:
                    continue
                seen[s] = v
                e.wait_ge(self.sems[s], v)
            if op.fn is None:
                continue
            ins = op.fn(e)
            if op.sig:
                ins.then_inc(self.sems[op.sem], 16 if op.dma else 1)


class Arena:
    def __init__(self, nc):
        self.nc = nc
        self.off = SB_BASE
        self.n = 0
        self.memo = None
        self.replay = None

    def start_group(self):
        self.memo = {}
        self.replay = None

    def next_member(self):
        self.replay = {k: list(v) for k, v in self.memo.items()}

    def end_group(self):
        self.memo = None
        self.replay = None

    def mark(self):
        return self.off

    def reset(self, m):
        self.off = m

    def alloc(self, shape, dtype):
        key = (tuple(shape), str(dtype))
        if self.replay is not None and self.replay.get(key):
            return self.replay[key].pop(0)
        t = self._alloc(shape, dtype)
        if self.memo is not None:
            self.memo.setdefault(key, []).append(t)
        return t

    def _alloc(self, shape, dtype):
        sz = 4 if dtype == F32 else 2
        nb = int(np.prod(shape[1:])) * sz
        nb = (nb + 31) // 32 * 32
        assert self.off + nb <= SB_LIMIT, ("SBUF overflow", self.off, nb)
        self.n += 1
        t = self.nc.alloc_sbuf_tensor_at("sb%d" % self.n, list(shape), dtype, offset=self.off)
        self.off += nb
        return t


def build_program():
    nc = bass.Bass("TRN2", target_bir_lowering=False)
    P = Prog()
    A = Arena(nc)

    def din(name, shape, dt=F32):
        return nc.dram_tensor(name, list(shape), dt, kind="ExternalInput")

    def dscr(name, shape, dt=BF16):
        return nc.dram_tensor(name, list(shape), dt, kind="Internal")

    xfa = din("xfa", [192, 128, KC, 128])
    tfa = din("tfa", [192, 128, 256], BF16)
    win_f = din("win_f", [128, KC, 1024])
    win_c = din("win_c", [64, 128, KC, 128])
    win_v = din("win_v", [2, 128, KC, 512])
    fw = din("fw", [128, 8, 2048])
    nw = din("nw", [16, 128, 8, 128])
    wout = din("wout", [16, 128, KC, 128])
    fup = din("fup", [88, 128, KC, 128])
    fdn = din("fdn", [16, 128, 44, 128])
    pg = din("pg", [16, 128, KC, 128])
    pp = din("pp", [16, 128, 2, 128])
    vecs = din("vecs", [128, 32 + 64 + 88 * 4])
    ccs = din("ccs", [128, 256])
    bias_t = din("bias_t", [128, 8, 1024], BF16)
    ident_f_d = din("ident_f", [128, 128])
    cbf_d = din("cbf", [128, 256], BF16)
    xk = [din("xk0", [128, KC, KVT[0]]), din("xk1", [128, KC, KVT[1]])]
    r2t_d = [din("r2t0", [128, 2, 2 * SEQ_NCC[0]], BF16), din("r2t1", [128, 2, 2 * SEQ_NCC[1]], BF16)]
    vb_d = [din("vb0", [128, NQT[0] * 16]), din("vb1", [128, NQT[1] * 16])]
    tm_d = [din("tm0", [128, E64[0]]), din("tm1", [128, E64[1]])]
    pT_d = [din("pT0", [128, 2, SEQ_OWN[0]]), din("pT1", [128, 2, SEQ_OWN[1]])]
    y_d = [nc.dram_tensor("y0", [SEQ_OWN[0], D], F32, kind="ExternalOutput"),
           nc.dram_tensor("y1", [SEQ_OWN[1], D], F32, kind="ExternalOutput")]

    Zs = [dscr("Zs0", [128, 128, 2048]), dscr("Zs1", [64, 128, 2048])]
    XTs = [dscr("XTs%d" % s, [128, 16, SEQ_NCC[s] * 128]) for s in range(2)]
    Ks = [dscr("Ks%d" % s, [128, 8, KVT[s]]) for s in range(2)]
    Vs = [dscr("Vs%d" % s, [128, KVT[s] // 128, 1024]) for s in range(2)]
    Qs = [dscr("Qs%d" % s, [128, 8, E64[s]]) for s in range(2)]
    Gs = [dscr("Gs%d" % s, [128, 16, 2, E64[s]]) for s in range(2)]
    As = [dscr("As%d" % s, [128, 8, E64[s]]) for s in range(2)]
    X1b = [dscr("X1b%d" % s, [128, KC, E64[s]]) for s in range(2)]
    X1f = [dscr("X1f%d" % s, [128, KC, E64[s]], F32) for s in range(2)]
    Hs = dscr("Hs", [128, 44, 3072])
    Wp = dscr("Wp", [16, 128, 16, 128])
    nwb = dscr("nwb", [16, 128, 8, 128])
    woutb = dscr("woutb", [16, 128, 16, 128])
    fdnb = dscr("fdnb", [16, 128, 44, 128])
    pgb = dscr("pgb", [16, 128, 16, 128])
    ppb = dscr("ppb", [16, 128, 2, 128])

    ps = [nc.alloc_psum_tensor("psb%d" % i, [128, 512], F32) for i in range(8)]
    dram_names = set()
    for t_ in (Zs + XTs + Ks + Vs + Qs + Gs + As + X1b + X1f + [Hs, Wp, nwb, woutb, fdnb, pgb, ppb] + y_d):
        dram_names.add(t_.name)

    def K(t, *i):
        return (t.name,) + tuple(i)

    def dma(out, in_, reads, writes, eng="sp"):
        reads = [k for k in reads if k[0] not in dram_names]
        writes = [k for k in writes if k[0] not in dram_names]
        P.add(eng, lambda e, o=out, i=in_: e.dma_start(out=o, in_=i), reads=reads, writes=writes, dma=True)

    vec_sb = A.alloc([128, 32 + 64 + 352], F32)
    ccs_sb = A.alloc([128, 256], F32)
    idf_sb = A.alloc([128, 128], F32)
    cbf_sb = A.alloc([128, 256], BF16)
    eps_sb = A.alloc([128, 1], F32)
    dummy = A.alloc([128, 8], F32)
    for (sb, dr) in ((vec_sb, vecs), (ccs_sb, ccs), (idf_sb, ident_f_d), (cbf_sb, cbf_d)):
        dma(sb[:], dr[:], [], [K(sb)])
    P.add("dve", lambda e: e.memset(eps_sb[:], 1e-5), writes=[K(eps_sb)])
    ident_b = cbf_sb[:, 0:128]
    ones_b = cbf_sb[:, 128:256]
    gb_sb = vec_sb[:, 0:32]
    ln_g = [vec_sb[:, 32:48], vec_sb[:, 64:80]]
    ln_b = [vec_sb[:, 48:64], vec_sb[:, 80:96]]
    CW0 = 96
    CB0 = 96 + 264
    pstg = [A.alloc([128, 2048], F32) for _ in range(1)]
    pstb = [A.alloc([128, 2048], BF16) for _ in range(2)]
    base_mark = A.mark()

    rot = {"i": 0}

    def alt(engs=("dve", "act")):
        rot["i"] += 1
        return engs[rot["i"] % len(engs)]

    def copy_op(eng, out, in_, reads, writes, scale=None):
        if eng == "act":
            if scale is None:
                P.add("act", lambda e: e.copy(out=out, in_=in_), reads=reads, writes=writes)
            else:
                P.add("act", lambda e: e.mul(out=out, in_=in_, mul=scale), reads=reads, writes=writes)
        else:
            if scale is None:
                P.add(eng, lambda e: e.tensor_copy(out=out, in_=in_), reads=reads, writes=writes)
            else:
                P.add(eng, lambda e: e.tensor_scalar(out=out, in0=in_, scalar1=scale, scalar2=None, op0=ALU.mult),
                      reads=reads, writes=writes)

    deferred = []

    def run_deferred(n=None):
        k = len(deferred) if n is None else min(n, len(deferred))
        for _ in range(k):
            deferred.pop(0)()

    prep_pieces = []
    for (src_, dst_) in ((nw, nwb), (wout, woutb), (pg, pgb), (fdn, fdnb)):
        for m_ in range(16):
            prep_pieces.append((src_, dst_, m_))

    def prep_dma(n):
        for _ in range(n):
            if not prep_pieces:
                return
            src_, dst_, m_ = prep_pieces.pop(0)
            dma(dst_[m_], src_[m_], [], [K(dst_)], eng="pool")

    def phase_prep():
        dma(ppb[:], pp[:], [], [K(ppb)], eng="pool")

    def prep_g(g):
        if True:
            s = 0
            dma(pstg[s][:], fw[:, g, :], [], [K(pstg[s])])
            for ri in range(2):
                s2 = ri
                for q in range(4):
                    b = ps[4 + q]
                    P.add("pe", lambda e, b=b, s=s, ri=ri, q=q: e.matmul(
                        b[:], lhsT=ccs_sb[:, ri * 128:(ri + 1) * 128], rhs=pstg[s][:, q * 512:(q + 1) * 512],
                        start=True, stop=True), reads=[K(pstg[s]), K(ccs_sb)], writes=[K(b)])
                    copy_op(alt(), pstb[s2][:, q * 512:(q + 1) * 512], b[:], [K(b)], [K(pstb[s2], q)])
                for m in range(16):
                    dma(Wp[m, :, g * 2 + ri, :], pstb[s2][:, m * 128:(m + 1) * 128], [K(pstb[s2], m // 4)], [K(Wp)], eng="act")

    def phase_a():
        wfb = A.alloc([128, KC, 1024], BF16)
        dma(wfb[:], win_f[:], [], [K(wfb)], eng="pool")
        NS = 3
        xb = [A.alloc([128, KC, 128], BF16) for _ in range(NS)]
        tt = [A.alloc([128, 256], BF16) for _ in range(NS)]
        fa = [A.alloc([128, 1024], BF16) for _ in range(2)]
        zt = [A.alloc([128, 2048], BF16) for _ in range(2)]

        def load(t):
            s = t % NS
            dma(xb[s][:], xfa[t], [], [K(xb[s])], eng="pool")
            dma(tt[s][:], tfa[t], [], [K(tt[s])])

        def proj(t):
            s, sf = t % NS, t % 2
            for half in range(2):
                b = ps[sf * 2 + half]
                for kc in range(KC):
                    P.add("pe", lambda e, b=b, s=s, kc=kc, half=half: e.matmul(
                        b[:], lhsT=xb[s][:, kc, :], rhs=wfb[:, kc, half * 512:(half + 1) * 512],
                        start=(kc == 0), stop=(kc == KC - 1)), reads=[K(xb[s]), K(wfb)], writes=[K(b)])
                copy_op(("act", "dve")[half], fa[sf][:, half * 512:(half + 1) * 512], b[:], [K(b)], [K(fa[sf], half)])

        def stage1(t):
            s, sf = t % NS, t % 2
            seq, a = (0, t) if t < 128 else (1, t - 128)
            for ri in range(2):
                for half in range(2):
                    b = ps[4 + ri * 2 + half]
                    P.add("pe", lambda e, b=b, s=s, sf=sf, ri=ri, half=half: e.matmul(
                        b[:], lhsT=tt[s][:, ri * 128:(ri + 1) * 128], rhs=fa[sf][:, half * 512:(half + 1) * 512],
                        start=True, stop=True), reads=[K(tt[s]), K(fa[sf], half)], writes=[K(b)])
                    o = ri * 1024 + half * 512
                    copy_op(("act", "dve")[(ri + half) % 2], zt[sf][:, o:o + 512], b[:], [K(b)], [K(zt[sf], ri * 2 + half)])
            dma(Zs[seq][a, :, :], zt[sf][:], [K(zt[sf], q) for q in range(4)], [K(Zs[seq])])

        load(0)
        load(1)
        for t in range(192):
            proj(t)
            if t >= 1:
                stage1(t - 1)
            if t + 2 < 192:
                load(t + 2)
            if t % 3 == 2:
                prep_dma(1)
            if t == 5:
                phase_prep()
        stage1(191)

    def phase_b(seq):
        Pn, ncc = SEQ_P[seq], SEQ_NCC[seq]
        N = 2 * ncc
        r2 = A.alloc([128, 2, N], BF16)
        dma(r2[:], r2t_d[seq][:], [], [K(r2)])
        xtt = A.alloc([128, 16, ncc, 128], BF16)
        DB = 4
        zd = [A.alloc([128, DB, 2048], BF16) for _ in range(3)]

        def load(i):
            dma(zd[i % 3][0:Pn, :, :], Zs[seq][:, i * DB:(i + 1) * DB, :], [K(Zs[seq])], [K(zd[i % 3])], eng=("sp", "pool")[i % 2])

        nld = 128 // DB
        load(0)
        load(1)
        for i in range(nld):
            s = i % 3
            for dd in range(DB):
                d = i * DB + dd
                b = ps[d % 8]
                for g in range(8):
                    P.add("pe", lambda e, b=b, s=s, g=g, dd=dd: e.matmul(
                        b[:, g * N:(g + 1) * N], lhsT=zd[s][0:Pn, dd, g * 128:(g + 1) * 128], rhs=r2[0:Pn, 0, :],
                        start=True, stop=False), reads=[K(zd[s]), K(r2)], writes=[K(b)])
                    P.add("pe", lambda e, b=b, s=s, g=g, dd=dd: e.matmul(
                        b[:, g * N:(g + 1) * N], lhsT=zd[s][0:Pn, dd, 1024 + g * 128:1024 + (g + 1) * 128], rhs=r2[0:Pn, 1, :],
                        start=False, stop=True), reads=[K(zd[s]), K(r2)], writes=[K(b)])
                copy_op(("dve", "act")[d % 2], xtt[:, :, :, d], b[:, 0:16 * ncc].rearrange("p (j c) -> p j c", c=ncc),
                        [K(b)], [K(xtt)])
            if i + 2 < nld:
                load(i + 2)
        for j in range(16):
            dma(XTs[seq][:, j, :], xtt[:, j, :, :].rearrange("p c d -> p (c d)"), [K(xtt)], [K(XTs[seq])], eng="act")

    def phase_c(seq):
        kvt, e64 = KVT[seq], E64[seq]
        src = A.alloc([128, KC, kvt], BF16)
        do_prep = (seq == 0)
        srck = [K(src, i) for i in range(kvt // 512)]
        NS = 3
        wb = [A.alloc([128, KC, 128], BF16) for _ in range(NS)]
        ost = [A.alloc([128, 512], BF16) for _ in range(6)]
        kv_tiles = [(o, min(512, kvt - o)) for o in range(0, kvt, 512)]
        q_tiles = [(KV_E0 + o, min(512, e64 - o)) for o in range(0, e64, 512)]
        jobs = [("k", h, 16 + h) for h in range(8)] + [("q", h, 8 + h) for h in range(8)] + \
               [("g", j, 32 + j) for j in range(32)]

        def load(ji):
            dma(wb[ji % NS][:], win_c[jobs[ji][2]], [], [K(wb[ji % NS])], eng="pool")

        vwb = A.alloc([128, KC, 512], BF16)
        nb = 0
        no = 0
        load(0)
        for i in range((kvt + 511) // 512):
            pw_ = min(512, kvt - i * 512)
            dma(src[:, :, i * 512:i * 512 + pw_], xk[seq][:, :, i * 512:i * 512 + pw_], [], [K(src, i)], eng="pool")
            if i == 0:
                load(1)
        for ji, (kind, idx, _c) in enumerate(jobs):
            s = ji % NS
            if ji + 2 < len(jobs):
                load(ji + 2)
            for (o, W) in (kv_tiles if kind == "k" else q_tiles):
                b = ps[nb % 6]
                nb += 1
                sk = [K(src, i_) for i_ in range(o // 512, (o + W - 1) // 512 + 1)]
                for kc in range(KC):
                    P.add("pe", lambda e, b=b, s=s, kc=kc, o=o, W=W: e.matmul(
                        b[:, 0:W], lhsT=wb[s][:, kc, :], rhs=src[:, kc, o:o + W], start=(kc == 0), stop=(kc == KC - 1)),
                        reads=[K(wb[s])] + sk, writes=[K(b)])
                os_ = ost[no % 6]
                no += 1
                if kind == "k":
                    copy_op(alt(), os_[:, 0:W], b[:, 0:W], [K(b)], [K(os_)])
                    dma(Ks[seq][:, idx, o:o + W], os_[:, 0:W], [K(os_)], [K(Ks[seq])])
                elif kind == "q":
                    copy_op(alt(), os_[:, 0:W], b[:, 0:W], [K(b)], [K(os_)], scale=128.0 ** -0.5)
                    dma(Qs[seq][:, idx, o - KV_E0:o - KV_E0 + W], os_[:, 0:W], [K(os_)], [K(Qs[seq])])
                else:
                    P.add("act", lambda e, b=b, os_=os_, W=W, idx=idx: e.activation(
                        out=os_[:, 0:W], in_=b[:, 0:W], func=AF.Sigmoid, bias=gb_sb[:, idx:idx + 1]),
                        reads=[K(b), K(vec_sb)], writes=[K(os_)])
                    dma(Gs[seq][:, idx % 16, idx // 16, o - KV_E0:o - KV_E0 + W], os_[:, 0:W], [K(os_)], [K(Gs[seq])])
        if do_prep:
            prep_dma(1000)
        for half in range(2):
            dma(vwb[:], win_v[half], [], [K(vwb)], eng="pool")
            for sub in range(kvt // 128):
                b = ps[nb % 6]
                nb += 1
                for kc in range(KC):
                    P.add("pe", lambda e, b=b, kc=kc, sub=sub: e.matmul(
                        b[:], lhsT=src[:, kc, sub * 128:(sub + 1) * 128], rhs=vwb[:, kc, :],
                        start=(kc == 0), stop=(kc == KC - 1)), reads=[K(src, sub // 4), K(vwb)], writes=[K(b)])
                os_ = ost[no % 6]
                no += 1
                copy_op(alt(), os_[:], b[:], [K(b)], [K(os_)])
                dma(Vs[seq][:, sub, half * 512:(half + 1) * 512], os_[:], [K(os_)], [K(Vs[seq])])

    def phase_d(seq):
        nq = NQT[seq]
        bt = A.alloc([128, 8, 1024], BF16)
        dma(bt[:], bias_t[:], [], [K(bt)])
        vb = A.alloc([128, nq * 16], F32)
        dma(vb[:], vb_d[seq][:], [], [K(vb)])
        qt = [A.alloc([128, 8, 128], BF16) for _ in range(2)]
        RK = 10
        kt = A.alloc([128, 8, RK * 128], BF16)
        vt = A.alloc([128, RK, 1024], BF16)
        npairs = KVT[seq] // 128
        pt = [A.alloc([128, 1024], BF16) for _ in range(2)]
        rd = A.alloc([128, 1024], F32)
        osb = A.alloc([128, 1024], F32)
        at = [A.alloc([128, 8, 128], BF16) for _ in range(2)]
        def load(i):
            s = i % 2
            dma(qt[s][:], Qs[seq][:, :, i * 128:(i + 1) * 128], [K(Qs[seq])], [K(qt[s])])

        def load_pair(p):
            if p >= npairs:
                return
            sl = p % RK
            dma(kt[:, :, sl * 128:(sl + 1) * 128], Ks[seq][:, :, p * 128:(p + 1) * 128], [], [K(kt, sl)])
            dma(vt[:, sl, :], Vs[seq][:, p, :], [], [K(vt, sl)])

        steps = []
        for i in range(nq):
            if i < 4:
                kps = list(range(1, 8))
            elif i >= nq - 3:
                kps = list(range(0, 6))
            else:
                kps = list(range(1, 6))
            for kp in kps:
                steps.append((i, kp, kps[0], kps[-1]))

        def score_exp(n):
            i, kp, k0, k1 = steps[n]
            s = i % 2
            sc = (ps[(n % 2) * 2], ps[(n % 2) * 2 + 1])
            p_ = pt[n % 2]
            for h in range(8):
                cs = slice(h * 64, (h + 1) * 64)
                for qb in range(2):
                    P.add("pe", lambda e, b=sc[qb], cs=cs, s=s, h=h, sl=(i + kp - 1) % RK, qb=qb: e.matmul(
                        b[:, cs], lhsT=kt[:, h, sl * 128:(sl + 1) * 128], rhs=qt[s][:, h, qb * 64:(qb + 1) * 64],
                        start=True, stop=False), reads=[K(kt, (i + kp - 1) % RK), K(qt[s])], writes=[K(sc[qb])])
                for qb in range(2):
                    P.add("pe", lambda e, b=sc[qb], cs=cs, h=h, kp=kp, qb=qb: e.matmul(
                        b[:, cs], lhsT=ident_b, rhs=bt[:, kp, h * 128 + qb * 64:h * 128 + qb * 64 + 64],
                        start=False, stop=True), reads=[K(bt), K(cbf_sb)], writes=[K(sc[qb])])
            for qb in range(2):
                col = (i * 8 + kp) * 2 + qb
                iv = sc[qb][:].rearrange("p (h q) -> p h q", h=8)
                ov = p_[:].rearrange("p (h b q) -> p h b q", h=8, b=2)[:, :, qb, :]
                P.add("act", lambda e, iv=iv, ov=ov, col=col: e.activation(
                    out=ov, in_=iv, func=AF.Exp, bias=vb[:, col:col + 1]),
                    reads=[K(sc[qb]), K(vb)], writes=[K(p_, qb)])

        def pv(n):
            i, kp, k0, k1 = steps[n]
            s = i % 2
            p_ = pt[n % 2]
            prk = [K(p_, 0), K(p_, 1)]
            for h in range(8):
                b = ps[4 + h // 4]
                cs = slice((h % 4) * 128, (h % 4 + 1) * 128)
                P.add("pe", lambda e, b=b, cs=cs, sl=(i + kp - 1) % RK, h=h, kp=kp, p_=p_, k0=k0, k1=k1: e.matmul(
                    b[:, cs], lhsT=vt[:, sl, h * 128:(h + 1) * 128], rhs=p_[:, h * 128:(h + 1) * 128],
                    start=(kp == k0 and h % 4 == 0), stop=(kp == k1)), reads=[K(vt, (i + kp - 1) % RK)] + prk, writes=[K(b)])
            for bk in range(2):
                b = ps[6 + bk]
                P.add("pe", lambda e, b=b, bk=bk, kp=kp, p_=p_, k0=k0, k1=k1: e.matmul(
                    b[:], lhsT=ones_b, rhs=p_[:, bk * 512:(bk + 1) * 512], start=(kp == k0), stop=(kp == k1)),
                    reads=prk + [K(cbf_sb)], writes=[K(b)])
            if kp == k1:
                for bk in range(2):
                    eng_ = ("dve", "act")[bk]
                    copy_op(eng_, osb[:, bk * 512:(bk + 1) * 512], ps[4 + bk][:], [K(ps[4 + bk])], [K(osb, bk)])
                    if eng_ == "dve":
                        P.add("dve", lambda e, bk=bk: e.reciprocal(out=rd[:, bk * 512:(bk + 1) * 512], in_=ps[6 + bk][:]),
                              reads=[K(ps[6 + bk])], writes=[K(rd, bk)])
                    else:
                        copy_op("act", rd[:, bk * 512:(bk + 1) * 512], ps[6 + bk][:], [K(ps[6 + bk])], [K(rd, bk)])
                        P.add("dve", lambda e, bk=bk: e.reciprocal(out=rd[:, bk * 512:(bk + 1) * 512], in_=rd[:, bk * 512:(bk + 1) * 512]),
                              reads=[K(rd, bk)], writes=[K(rd, bk)])
                for bk in range(2):
                    P.add(("dve", "pool")[bk], lambda e, bk=bk, s=s: e.tensor_tensor(
                        out=at[s][:, bk * 4:(bk + 1) * 4, :].rearrange("p h q -> p (h q)"), in0=osb[:, bk * 512:(bk + 1) * 512],
                        in1=rd[:, bk * 512:(bk + 1) * 512], op=ALU.mult), reads=[K(osb, bk), K(rd, bk)], writes=[K(at[s], bk)])
                dma(As[seq][:, :, i * 128:(i + 1) * 128], at[s][:], [K(at[s], 0), K(at[s], 1)], [K(As[seq])])

        load(0)
        for p_i in range(8):
            load_pair(p_i)
        if nq > 1:
            load(1)
        load_pair(8)
        load_pair(9)
        score_exp(0)
        for n in range(len(steps)):
            i, kp, k0, k1 = steps[n]
            if n + 1 < len(steps):
                score_exp(n + 1)
            pv(n)
            if kp == k1:
                if i + 2 < nq:
                    load(i + 2)
                if i >= 1:
                    load_pair(i - 1 + RK)

    def ln_chunk_stats(pre, kc, W, scr_b, scr_q, sq_eng="act"):
        P.add("act", lambda e: e.copy(out=scr_b[:, kc, 0:W], in_=pre[:, kc, 0:W]), reads=[K(pre, kc)], writes=[K(scr_b, kc)])
        if sq_eng == "act":
            P.add("act", lambda e: e.activation(out=scr_q[:, kc, 0:W], in_=pre[:, kc, 0:W], func=AF.Square), reads=[K(pre, kc)], writes=[K(scr_q, kc)])
        else:
            P.add("dve", lambda e: e.tensor_tensor(out=scr_q[:, kc, 0:W], in0=pre[:, kc, 0:W], in1=pre[:, kc, 0:W], op=ALU.mult),
                  reads=[K(pre, kc)], writes=[K(scr_q, kc)])

    def ln_thunks(pre, W, which, scr_b, scr_q, st, out_of, fin, sq_eng="act", mul_eng="pool", split=False):
        mean, msq, var, rstd = st
        pk = [K(pre, kc) for kc in range(KC)]
        th = []

        sbk = [K(scr_b, kc) for kc in range(KC)]
        sqk = [K(scr_q, kc) for kc in range(KC)]

        def stats2():
            for kc in range(KC):
                P.add("pe", lambda e, kc=kc: e.matmul(ps[6][:, 0:W], lhsT=ones_b, rhs=scr_b[:, kc, 0:W],
                                                      start=(kc == 0), stop=(kc == KC - 1)),
                      reads=[K(scr_b, kc), K(cbf_sb)], writes=[K(ps[6])])
            for kc in range(KC):
                P.add("pe", lambda e, kc=kc: e.matmul(ps[7][:, 0:W], lhsT=ones_b, rhs=scr_q[:, kc, 0:W],
                                                      start=(kc == 0), stop=(kc == KC - 1)),
                      reads=[K(scr_q, kc), K(cbf_sb)], writes=[K(ps[7])])
            P.add("dve", lambda e: e.tensor_scalar(out=mean[:, 0:W], in0=ps[6][:, 0:W], scalar1=1.0 / D, scalar2=None, op0=ALU.mult),
                  reads=[K(ps[6])], writes=[K(mean)])
            P.add("dve", lambda e: e.tensor_tensor(out=msq[:, 0:W], in0=mean[:, 0:W], in1=mean[:, 0:W], op=ALU.mult),
                  reads=[K(mean)], writes=[K(msq)])
            P.add("dve", lambda e: e.scalar_tensor_tensor(out=var[:, 0:W], in0=ps[7][:, 0:W], scalar=1.0 / D, in1=msq[:, 0:W],
                                                          op0=ALU.mult, op1=ALU.subtract),
                  reads=[K(ps[7]), K(msq)], writes=[K(var)])
            P.add("act", lambda e: e.activation(out=var[:, 0:W], in_=var[:, 0:W], func=AF.Sqrt, bias=eps_sb[:, 0:1]),
                  reads=[K(var), K(eps_sb)], writes=[K(var)])
            P.add("dve", lambda e: e.reciprocal(out=rstd[:, 0:W], in_=var[:, 0:W]), reads=[K(var)], writes=[K(rstd)])

        th.append(stats2)

        def norm(kc, do_fin=True):
            o = out_of(kc)
            ok = o[1]
            P.add("dve", lambda e: e.tensor_tensor(out=o[0], in0=pre[:, kc, 0:W], in1=mean[:, 0:W], op=ALU.subtract),
                  reads=[K(pre, kc), K(mean), K(rstd)], writes=[ok])
            P.add(mul_eng, lambda e: e.tensor_tensor(out=o[0], in0=o[0], in1=rstd[:, 0:W], op=ALU.mult),
                  reads=[ok, K(rstd)], writes=[ok])
            P.add("act", lambda e: e.activation(out=o[0], in_=o[0], func=AF.Identity,
                                                bias=ln_b[which][:, kc:kc + 1], scale=ln_g[which][:, kc:kc + 1]),
                  reads=[ok, K(vec_sb)], writes=[ok])
            if do_fin:
                fin(kc)

        if not split:
            for kc in range(KC):
                th.append(lambda kc=kc: norm(kc))
        else:
            for kc in range(KC):
                th.append(lambda kc=kc: norm(kc, False))
                if kc >= 1:
                    th.append(lambda kc=kc: fin(kc - 1))
            th.append(lambda: fin(KC - 1))
        return th

    def phase_e(seq):
        e64 = E64[seq]
        nt_ = (e64 + 511) // 512
        wbase = ((e64 // nt_) + 63) // 64 * 64
        tiles = []
        o_ = 0
        while o_ < e64:
            tiles.append((o_, min(wbase, e64 - o_)))
            o_ += wbase
        xt = A.alloc([128, 16, 512], BF16)
        att = A.alloc([128, 8, 512], BF16)
        gt = [A.alloc([128, 2, 512], BF16) for _ in range(4)]
        mg = A.alloc([128, 16, 512], BF16)
        pre = A.alloc([128, 16, 512], F32)
        xres = [A.alloc([128, 512], F32) for _ in range(4)]
        scr_b = A.alloc([128, 16, 512], BF16)
        scr_q = A.alloc([128, 16, 512], BF16)
        st = [A.alloc([128, 512], F32) for _ in range(3)]
        st = [st[0], st[2], st[1], st[2]]
        t1 = [A.alloc([128, 512], F32) for _ in range(3)]
        t2 = [A.alloc([128, 512], F32) for _ in range(3)]
        tmk = [A.alloc([128, 512], F32) for _ in range(2)]
        x1o = [A.alloc([128, 512], F32) for _ in range(3)]
        x1ob = [A.alloc([128, 512], BF16) for _ in range(3)]
        wa = [A.alloc([128, 16, 128], BF16) for _ in range(4)]
        wn = [A.alloc([128, 8, 128], BF16) for _ in range(4)]
        wo = wa
        cnt = {"n": 0, "o": 0, "w": 0}

        for ti, (o, W) in enumerate(tiles):
            dma(xt[:, :, 0:W], XTs[seq][:, :, 64 + o:64 + o + W], [K(XTs[seq])], [K(xt)])
            dma(att[:, :, 0:W], As[seq][:, :, o:o + W], [K(As[seq])], [K(att)])

            def loadw(m, o=o, W=W):
                s = m % 4
                dma(wa[s][:], Wp[m], [K(Wp)], [K(wa[s])])
                dma(wn[s][:], nwb[m], [K(nwb)], [K(wn[s])])
                dma(gt[s][:, :, 0:W], Gs[seq][:, m, :, o:o + W], [K(Gs[seq])], [K(gt[s])])

            loadw(0)
            loadw(1)
            for m in range(16):
                s = m % 4
                n = cnt["n"]
                cnt["n"] += 1
                k2 = n % 3
                ba, bb = ps[k2 * 2], ps[k2 * 2 + 1]
                for j in range(16):
                    P.add("pe", lambda e, ba=ba, s=s, j=j, W=W: e.matmul(ba[:, 0:W], lhsT=wa[s][:, j, :], rhs=xt[:, j, 0:W],
                                                                         start=(j == 0), stop=(j == 15)),
                          reads=[K(wa[s]), K(xt)], writes=[K(ba)])
                for j in range(8):
                    P.add("pe", lambda e, bb=bb, s=s, j=j, W=W: e.matmul(bb[:, 0:W], lhsT=wn[s][:, j, :], rhs=att[:, j, 0:W],
                                                                         start=(j == 0), stop=(j == 7)),
                          reads=[K(wn[s]), K(att)], writes=[K(bb)])
                P.add("dve", lambda e, ba=ba, s=s, k2=k2, W=W: e.tensor_tensor(out=t1[k2][:, 0:W], in0=ba[:, 0:W], in1=gt[s][:, 0, 0:W], op=ALU.mult),
                      reads=[K(ba), K(gt[s])], writes=[K(t1[k2])])
                P.add("dve", lambda e, bb=bb, s=s, k2=k2, W=W: e.tensor_tensor(out=t2[k2][:, 0:W], in0=bb[:, 0:W], in1=gt[s][:, 1, 0:W], op=ALU.mult),
                      reads=[K(bb), K(gt[s])], writes=[K(t2[k2])])
                P.add("dve", lambda e, k2=k2, m=m, W=W: e.tensor_tensor(out=mg[:, m, 0:W], in0=t1[k2][:, 0:W], in1=t2[k2][:, 0:W], op=ALU.add),
                      reads=[K(t1[k2]), K(t2[k2])], writes=[K(mg, m)])
                if m + 2 < 16:
                    loadw(m + 2)
                run_deferred(2 if m == 0 else 1)
            run_deferred()
            mgk = [K(mg, m) for m in range(16)]

            def loado(m, o=o, W=W):
                s = m % 4
                dma(wo[s][:], woutb[m], [K(woutb)], [K(wo[s])])
                dma(xres[s][:, 0:W], xk[seq][:, m, KV_E0 + o:KV_E0 + o + W], [], [K(xres[s])])

            loado(0)
            loado(1)
            dma(tmk[ti % 2][:, 0:W], tm_d[seq][:, o:o + W], [], [K(tmk[ti % 2])])
            for m in range(16):
                s = m % 4
                b = ps[4 + m % 4]
                for j in range(16):
                    P.add("pe", lambda e, b=b, s=s, j=j, W=W: e.matmul(b[:, 0:W], lhsT=wo[s][:, j, :], rhs=mg[:, j, 0:W],
                                                                       start=(j == 0), stop=(j == 15)),
                          reads=[K(wo[s]), K(mg, j)], writes=[K(b)])
                P.add("dve", lambda e, b=b, m=m, s=s, W=W: e.scalar_tensor_tensor(out=pre[:, m, 0:W], in0=xres[s][:, 0:W], scalar=ALPHA, in1=b[:, 0:W],
                                                                                  op0=ALU.mult, op1=ALU.add),
                      reads=[K(b), K(xres[s])], writes=[K(pre, m)])
                ln_chunk_stats(pre, m, W, scr_b, scr_q, "act")
                if m + 2 < 16:
                    loado(m + 2)

            def out_of(kc, W=W):
                i = cnt["o"] % 3
                return (x1o[i][:, 0:W], K(x1o[i]), i)

            def fin(kc, o=o, W=W, ti=ti, tail=(seq == 1 and ti == len(tiles) - 1)):
                mk_eng = "pool" if (tail and kc % 2 == 0) else "dve"
                i = cnt["o"] % 3
                cnt["o"] += 1
                dma(X1f[seq][:, kc, o:o + W], x1o[i][:, 0:W], [K(x1o[i])], [K(X1f[seq])], eng="act")
                P.add(mk_eng, lambda e: e.tensor_tensor(out=x1ob[i][:, 0:W], in0=x1o[i][:, 0:W], in1=tmk[ti % 2][:, 0:W], op=ALU.mult),
                      reads=[K(x1o[i]), K(tmk[ti % 2])], writes=[K(x1ob[i])])
                dma(X1b[seq][:, kc, o:o + W], x1ob[i][:, 0:W], [K(x1ob[i])], [K(X1b[seq])], eng=("pool" if mk_eng == "pool" else "act"))

            last_ = (seq == 1 and ti == len(tiles) - 1)
            deferred.extend(ln_thunks(pre, W, 0, scr_b, scr_q, st, out_of, fin, mul_eng=("dve" if last_ else "pool")))
        if seq == 1:
            run_deferred()

    def phase_f():
        res = [A.alloc([128, KC, SEQ_OWN[s] + 2], BF16) for s in range(2)]
        for s in range(2):
            nres = SEQ_OWN[s] + 2
            for pi, po in enumerate(range(0, nres, 512)):
                pw = min(512, nres - po)
                dma(res[s][:, :, po:po + pw], X1b[s][:, :, 63 + po:63 + po + pw], [], [K(res[s], pi)])
        NS = 6
        wb = [A.alloc([128, KC, 128], BF16) for _ in range(NS)]
        ug = [A.alloc([128, 512], F32) for _ in range(2)]
        ul = [A.alloc([128, 512], F32) for _ in range(2)]
        gg = [A.alloc([128, 512], F32) for _ in range(2)]
        hh = [A.alloc([128, 512], BF16) for _ in range(3)]
        tiles = []
        for s in range(2):
            ntl = (SEQ_OWN[s] + 509) // 510
            wbase = (SEQ_OWN[s] + ntl - 1) // ntl
            o_ = 0
            while o_ < SEQ_OWN[s]:
                tiles.append((s, o_, min(wbase, SEQ_OWN[s] - o_)))
                o_ += wbase

        def load(j):
            for part in range(2):
                sw = (j * 2 + part) % NS
                dma(wb[sw][:], fup[part * 44 + j], [], [K(wb[sw])], eng="pool")

        nt = 0
        nh = 0
        load(0)
        load(1)
        for j in range(44):
            if j + 2 < 44:
                load(j + 2)
            wsl = [(j * 2) % NS, (j * 2 + 1) % NS]
            for (s, o, W) in tiles:
                k = nt % 2
                nt += 1
                bg, bl = ps[k * 2], ps[k * 2 + 1]
                for part, b in ((0, bg), (1, bl)):
                    for kc in range(KC):
                        P.add("pe", lambda e, b=b, sw=wsl[part], kc=kc, s=s, o=o, W=W: e.matmul(
                            b[:, 0:W + 2], lhsT=wb[sw][:, kc, :], rhs=res[s][:, kc, o:o + W + 2],
                            start=(kc == 0), stop=(kc == KC - 1)),
                            reads=[K(wb[wsl[part]])] + [K(res[s], pi_) for pi_ in range(o // 512, (o + W + 1) // 512 + 1)], writes=[K(b)])
                for part, b, u in ((0, bg, ug[k]), (1, bl, ul[k])):
                    jj = part * 44 + j
                    w0 = vec_sb[:, CW0 + jj * 3 + 0:CW0 + jj * 3 + 1]
                    w1 = vec_sb[:, CW0 + jj * 3 + 1:CW0 + jj * 3 + 2]
                    w2 = vec_sb[:, CW0 + jj * 3 + 2:CW0 + jj * 3 + 3]
                    cb = vec_sb[:, CB0 + jj:CB0 + jj + 1]
                    P.add("act", lambda e, b=b, u=u, w1=w1, cb=cb, W=W: e.activation(
                        out=u[:, 0:W], in_=b[:, 1:W + 1], func=AF.Identity, bias=cb, scale=w1),
                        reads=[K(b), K(vec_sb)], writes=[K(u)])
                    P.add("dve", lambda e, b=b, u=u, w0=w0, W=W: e.scalar_tensor_tensor(
                        out=u[:, 0:W], in0=b[:, 0:W], scalar=w0, in1=u[:, 0:W], op0=ALU.mult, op1=ALU.add),
                        reads=[K(b), K(u), K(vec_sb)], writes=[K(u)])
                    P.add("dve", lambda e, b=b, u=u, w2=w2, W=W: e.scalar_tensor_tensor(
                        out=u[:, 0:W], in0=b[:, 2:W + 2], scalar=w2, in1=u[:, 0:W], op0=ALU.mult, op1=ALU.add),
                        reads=[K(b), K(u), K(vec_sb)], writes=[K(u)])
                P.add("act", lambda e, k=k, W=W: e.activation(out=gg[k][:, 0:W], in_=ug[k][:, 0:W], func=AF.Gelu_apprx_tanh),
                      reads=[K(ug[k])], writes=[K(gg[k])])
                h_ = hh[nh % 3]
                nh += 1
                P.add("dve", lambda e, k=k, W=W, h_=h_: e.tensor_tensor(out=h_[:, 0:W], in0=gg[k][:, 0:W], in1=ul[k][:, 0:W], op=ALU.mult),
                      reads=[K(gg[k]), K(ul[k])], writes=[K(h_)])
                oo = OWN_OFF[s] + o
                dma(Hs[:, j, oo:oo + W], h_[:, 0:W], [K(h_)], [K(Hs)])

    def phase_g():
        ht = A.alloc([128, 44, 512], BF16)
        xb1 = A.alloc([128, 16, 512], BF16)
        pb = A.alloc([128, 2, 512], BF16)
        pre = A.alloc([128, 16, 512], F32)
        xres = [A.alloc([128, 512], F32) for _ in range(2)]
        scr_b = A.alloc([128, 16, 512], BF16)
        scr_q = A.alloc([128, 16, 512], BF16)
        st = [A.alloc([128, 512], F32) for _ in range(3)]
        st = [st[0], st[2], st[1], st[2]]
        sg = [A.alloc([128, 512], F32) for _ in range(2)]
        wd = [A.alloc([128, 44, 128], BF16) for _ in range(2)]
        wg = [A.alloc([128, 16, 128], BF16) for _ in range(2)]
        wp = [A.alloc([128, 2, 128], BF16) for _ in range(2)]
        yc = [A.alloc([128, 512], F32) for _ in range(3)]
        yo = [A.alloc([128, 512], F32) for _ in range(3)]
        W = 512
        cnt = {"n": 0, "o": 0, "y": 0}
        for s in range(2):
            for o in range(0, SEQ_OWN[s], 512):
                oo = OWN_OFF[s] + o
                def loadw(m, s=s, o=o):
                    k = m % 2
                    dma(wd[k][:], fdnb[m], [K(fdnb)], [K(wd[k])])
                    dma(wg[k][:], pgb[m], [K(pgb)], [K(wg[k])])
                    dma(wp[k][:], ppb[m], [K(ppb)], [K(wp[k])])
                    dma(xres[k][:], X1f[s][:, m, 64 + o:64 + o + W], [K(X1f[s])], [K(xres[k])])

                loadw(0)
                for q_ in range(4):
                    dma(ht[:, q_ * 11:(q_ + 1) * 11, :], Hs[:, q_ * 11:(q_ + 1) * 11, oo:oo + W], [K(Hs)], [K(ht, q_)])
                dma(xb1[:], X1b[s][:, :, 64 + o:64 + o + W], [K(X1b[s])], [K(xb1)])
                dma(pb[:], pT_d[s][:, :, o:o + W], [], [K(pb)], eng="pool")
                for m in range(16):
                    k = m % 2
                    if m + 1 < 16:
                        loadw(m + 1)
                    n = cnt["n"]
                    cnt["n"] += 1
                    k2 = n % 2
                    b0, b1, b2 = ps[k2 * 3], ps[k2 * 3 + 1], ps[k2 * 3 + 2]
                    for j in range(44):
                        P.add("pe", lambda e, b0=b0, k=k, j=j: e.matmul(b0[:], lhsT=wd[k][:, j, :], rhs=ht[:, j, :],
                                                                        start=(j == 0), stop=(j == 43)),
                              reads=[K(wd[k]), K(ht, j // 11)], writes=[K(b0)])
                    for j in range(16):
                        P.add("pe", lambda e, b1=b1, k=k, j=j: e.matmul(b1[:], lhsT=wg[k][:, j, :], rhs=xb1[:, j, :],
                                                                        start=(j == 0), stop=(j == 15)),
                              reads=[K(wg[k]), K(xb1)], writes=[K(b1)])
                    for j in range(2):
                        P.add("pe", lambda e, b2=b2, k=k, j=j: e.matmul(b2[:], lhsT=wp[k][:, j, :], rhs=pb[:, j, :],
                                                                        start=(j == 0), stop=(j == 1)),
                              reads=[K(wp[k]), K(pb)], writes=[K(b2)])
                    run_deferred(2)
                    P.add("act", lambda e, b1=b1, k2=k2: e.activation(out=sg[k2][:], in_=b1[:], func=AF.Sigmoid),
                          reads=[K(b1)], writes=[K(sg[k2])])
                    P.add("dve", lambda e, b2=b2, k2=k2: e.tensor_tensor(out=sg[k2][:], in0=b2[:], in1=sg[k2][:], op=ALU.mult),
                          reads=[K(b2), K(sg[k2])], writes=[K(sg[k2])])
                    P.add("dve", lambda e, b0=b0, k2=k2: e.tensor_tensor(out=sg[k2][:], in0=b0[:], in1=sg[k2][:], op=ALU.add),
                          reads=[K(b0), K(sg[k2])], writes=[K(sg[k2])])
                    P.add("dve", lambda e, k=k, k2=k2, m=m: e.scalar_tensor_tensor(out=pre[:, m, :], in0=xres[k][:], scalar=ALPHA, in1=sg[k2][:],
                                                                                  op0=ALU.mult, op1=ALU.add),
                          reads=[K(sg[k2]), K(xres[k])], writes=[K(pre, m)])
                    ln_chunk_stats(pre, m, W, scr_b, scr_q, "dve")

                run_deferred()

                def out_of(kc):
                    i = kc % 3
                    return (yc[i][:], K(yc[i]), i)

                def fin(kc, s=s, o=o, ev_eng=("dve" if (s == 1 and o + 512 >= SEQ_OWN[s]) else "act")):
                    i = kc % 3
                    b = ps[6 + kc % 2]
                    for sub in range(4):
                        P.add("pe", lambda e, b=b, sub=sub, i=i: e.matmul(
                            b[:, sub * 128:(sub + 1) * 128], lhsT=yc[i][:, sub * 128:(sub + 1) * 128], rhs=idf_sb[:],
                            start=True, stop=True), reads=[K(yc[i]), K(idf_sb)], writes=[K(b)])
                    y_ = yo[cnt["y"] % 3]
                    cnt["y"] += 1
                    copy_op(ev_eng, y_[:], b[:], [K(b)], [K(y_)])
                    for sub in range(4):
                        dma(y_d[s][o + sub * 128:o + (sub + 1) * 128, kc * 128:(kc + 1) * 128], y_[:, sub * 128:(sub + 1) * 128],
                            [K(y_)], [K(y_d[s])], eng=("sp" if ev_eng == "dve" else "act"))

                last_ = (s == 1 and o + 512 >= SEQ_OWN[s])
                deferred.extend(ln_thunks(pre, W, 1, scr_b, scr_q, st, out_of, fin, sq_eng="dve", mul_eng=("dve" if last_ else "pool"), split=True))
        run_deferred()

    def run_phase(fn, *a, barrier=True):
        A.reset(base_mark)
        fn(*a)
        if barrier:
            P.barrier()

    P.barrier()
    for g_ in range(8):
        prep_g(g_)
    run_phase(phase_a)
    def run_group(fn):
        A.reset(base_mark)
        A.start_group()
        for s in range(2):
            if s > 0:
                A.next_member()
            fn(s)
        A.end_group()
        P.barrier()

    run_group(phase_b)
    for s in range(2):
        run_phase(phase_c, s)
    run_group(phase_d)
    run_group(phase_e)
    run_phase(phase_f)
    run_phase(phase_g)

    import contextlib
    with contextlib.ExitStack() as es:
        sems = {}
        for e_ in Prog.ENGS:
            sems[("c", e_)] = es.enter_context(nc.semaphore("c_" + e_))
            for k in range(Prog.KPOOL):
                sems[(e_, k)] = es.enter_context(nc.semaphore("d_%s%d" % (e_, k)))
        P.finalize(nc, sems)
        block = es.enter_context(nc.Block())

        @block.sync
        def _(e):
            P.emit("sp", e)

        @block.tensor
        def _(e):
            P.emit("pe", e)

        @block.vector
        def _(e):
            P.emit("dve", e)

        @block.scalar
        def _(e):
            P.emit("act", e)

        @block.gpsimd
        def _(e):
            P.emit("pool", e)
    return nc


def _fm(w):
    k, n = w.shape
    return np.ascontiguousarray(w.reshape(k // 128, 128, n).transpose(1, 0, 2))


def _chunks(w, cw=128):
    k, n = w.shape
    return np.ascontiguousarray(w.reshape(k // 128, 128, n // cw, cw).transpose(2, 1, 0, 3))


def _vec(v):
    return np.ascontiguousarray(v.reshape(-1, 128).T)


def _host_prep(inp):
    f32 = np.float32
    xs = [np.asarray(inp["x_prompt"], f32)[0], np.asarray(inp["x_sample"], f32)[0]]
    ps_ = [np.asarray(inp["p_prompt"], f32)[0, 0], np.asarray(inp["p_sample"], f32)[0, 0]]
    shared = {}
    xa0 = xs[0].reshape(128, 128, KC, 128).transpose(1, 3, 2, 0)
    xa1 = xs[1].reshape(128, 64, KC, 128).transpose(1, 3, 2, 0)
    shared["xfa"] = np.ascontiguousarray(np.concatenate([xa0, xa1], 0))
    tf = np.zeros((192, 128, 256), np.float64)
    bb = np.arange(128)[:, None]
    dd = np.arange(128)[None, :]
    for a in range(128):
        s = a + 128 * bb
        th = 2 * np.pi * ((s * dd) % 16384) / 16384.0
        tf[a, :, :128] = np.cos(th) / 128.0
        tf[a, :, 128:] = -np.sin(th) / 128.0
    for a in range(64):
        s = a + 64 * bb
        th = 2 * np.pi * ((s * dd) % 8192) / 8192.0
        tf[128 + a, :, :128] = np.cos(th) / np.sqrt(8192.0)
        tf[128 + a, :, 128:] = -np.sin(th) / np.sqrt(8192.0)
    shared["tfa"] = tf.astype(NPBF)
    w_in = np.asarray(inp["w_in"], f32)[0]
    shared["win_f"] = _fm(w_in[:, 0:1024])
    shared["win_c"] = _chunks(w_in)
    shared["win_v"] = _chunks(w_in[:, 3072:4096], 512)
    shared["fw"] = _fm(np.asarray(inp["fourier_w"], f32)[0])
    shared["nw"] = _chunks(np.asarray(inp["natten_w"], f32)[0])
    shared["wout"] = _chunks(np.asarray(inp["w_out"], f32)[0])
    shared["fup"] = _chunks(np.asarray(inp["ffn_up"], f32)[0])
    shared["fdn"] = _chunks(np.asarray(inp["ffn_down"], f32)[0])
    shared["pg"] = _chunks(np.asarray(inp["ple_gate"], f32)[0])
    shared["pp"] = _chunks(np.asarray(inp["ple_proj"], f32)[0])
    cw = np.asarray(inp["ffn_conv"], f32)[0]
    cwl = cw.T.reshape(88, 128, 3).transpose(1, 0, 2).reshape(128, 264)
    vecs = np.concatenate([
        _vec(np.asarray(inp["gate_b"], f32)[0]),
        _vec(np.asarray(inp["ln1_g"], f32)[0]), _vec(np.asarray(inp["ln1_b"], f32)[0]),
        _vec(np.asarray(inp["ln2_g"], f32)[0]), _vec(np.asarray(inp["ln2_b"], f32)[0]),
        cwl, _vec(np.asarray(inp["ffn_conv_b"], f32)[0])], axis=1)
    shared["vecs"] = np.ascontiguousarray(vecs.astype(f32))
    ch = np.arange(128)
    ang = 2 * np.pi * ((ch[:, None] * ch[None, :]) % 128) / 128.0
    shared["ccs"] = np.concatenate([np.cos(ang), np.sin(ang)], 1).astype(f32) / np.float32(np.sqrt(128.0))
    rpb = np.asarray(inp["natten_rpb"], f32)[0]
    bt = np.full((8, 128, 8, 2, 64), NEG, f32)
    qc = np.arange(64)
    cs = np.clip(qc - 8, 0, 48)
    for kp in range(8):
        for a in range(2):
            for b in range(2):
                dr = 2 * kp + a - 7 - b
                if abs(dr) > 7:
                    continue
                for kc_ in range(64):
                    ok = (kc_ >= cs) & (kc_ < cs + 16)
                    dc = kc_ - qc
                    vals = rpb[:, dr + 7, np.clip(dc + 15, 0, 30)]
                    bt[kp, a * 64 + kc_, :, b, :] = np.where(ok[None, :], vals, NEG)
    shared["bias_t"] = np.ascontiguousarray(bt.reshape(8, 128, 1024).transpose(1, 0, 2)).astype(NPBF)
    shared["ident_f"] = np.eye(128, dtype=f32)
    shared["cbf"] = np.concatenate([np.eye(128), np.ones((128, 128))], 1).astype(NPBF)

    maps = []
    for c in range(NCORES):
        m = dict(shared)
        for s in range(2):
            S, own, rows, Pn, ncc = SEQ_S[s], SEQ_OWN[s], SEQ_ROWS[s], SEQ_P[s], SEQ_NCC[s]
            t0 = c * own
            lo = t0 - 384
            hi = lo + KVT[s]
            xk = np.zeros((KVT[s], D), f32)
            a0, a1 = max(lo, 0), min(hi, S)
            xk[a0 - lo:a1 - lo] = xs[s][a0:a1]
            m["xk%d" % s] = np.ascontiguousarray(xk.T.reshape(KC, 128, KVT[s]).transpose(1, 0, 2))
            c0 = t0 // 128
            cc = np.arange(c0 - 1, c0 - 1 + ncc)
            aa = np.arange(Pn)
            ang = 2 * np.pi * ((aa[:, None] * cc[None, :]) % Pn) / float(Pn)
            r = np.zeros((128, 2, 2 * ncc), np.float64)
            r[:Pn, 0, :ncc] = np.cos(ang)
            r[:Pn, 0, ncc:] = -np.sin(ang)
            r[:Pn, 1, :ncc] = np.sin(ang)
            r[:Pn, 1, ncc:] = np.cos(ang)
            m["r2t%d" % s] = r.astype(NPBF)
            nq = NQT[s]
            r0 = t0 // 64
            vb = np.zeros((128, nq * 16), f32)
            for i in range(nq):
                for kp in range(8):
                    for b in range(2):
                        rq = r0 - 1 + 2 * i + b
                        col = (i * 8 + kp) * 2 + b
                        for a in range(2):
                            kr = r0 - 8 + 2 * (i + kp) + a
                            if rq < 0 or rq >= rows:
                                ok = True
                            else:
                                rs = min(max(rq - 4, 0), rows - 8)
                                ok = (rs <= kr < rs + 8)
                            vb[a * 64:(a + 1) * 64, col] = 0.0 if ok else NEG
            m["vb%d" % s] = vb
            tok = np.arange(t0 - 64, t0 - 64 + E64[s])
            tm = ((tok >= 0) & (tok < S)).astype(f32)
            m["tm%d" % s] = np.ascontiguousarray(np.broadcast_to(tm[None, :], (128, E64[s])))
            pt = ps_[s][t0:t0 + own]
            m["pT%d" % s] = np.ascontiguousarray(pt.T.reshape(2, 128, own).transpose(1, 0, 2))
        maps.append(m)
    return maps


_NC_CACHE = {}


def kernel(**inputs):
    maps = _host_prep(inputs)
    if "nc" not in _NC_CACHE:
        _NC_CACHE["nc"] = build_program()
    nc = _NC_CACHE["nc"]
    res = run_bass_kernel_spmd(nc, maps, core_ids=list(range(NCORES)))
    y0 = np.concatenate([np.asarray(res.results[c]["y0"], np.float32) for c in range(NCORES)], 0)[None]
    y1 = np.concatenate([np.asarray(res.results[c]["y1"], np.float32) for c in range(NCORES)], 0)[None]
    return (y0, y1)
```
